# Optimizing a Trainium2 kernel written in Bass

```python
import jax, jax.numpy as jnp
from jax import lax
import numpy as np

D_MODEL = 2048
BATCH = 1
SEQ = 8192
DEPTH = 2
DEC_BATCH = 8
DEC_SEQ = 64
PAST_LEN = 2048

CHUNK = 64
POOL_WIDTH = D_MODEL // 2
POOL_WINDOWS = (2, 4, 8, 16)
N_POOL_GROUPS = len(POOL_WINDOWS)
POOL_GROUP = POOL_WIDTH // N_POOL_GROUPS
POOL_HIST = max(POOL_WINDOWS) - 1
CONV_WIDTH = D_MODEL // 2
CONV_K = 31
CONV_HIST = CONV_K - 1
SPLITS = [POOL_WIDTH, 2 * POOL_WIDTH, 2 * POOL_WIDTH + 2 * CONV_WIDTH,
          2 * POOL_WIDTH + 3 * CONV_WIDTH, 2 * POOL_WIDTH + 3 * CONV_WIDTH + D_MODEL]
IN_COLS = 2 * POOL_WIDTH + 3 * CONV_WIDTH + 2 * D_MODEL
RMS_EPS = 1e-6
LN_EPS = 1e-5

kernel_name = "gated_pool_conformer_stream_step"


def rmsnorm(x, g):
    xf = x.astype(jnp.float32)
    y = xf * lax.rsqrt(jnp.mean(xf * xf, axis=-1, keepdims=True) + RMS_EPS)
    return (y * g.astype(jnp.float32)).astype(x.dtype)


def layernorm(x, g, b):
    xf = x.astype(jnp.float32)
    mu = jnp.mean(xf, axis=-1, keepdims=True)
    xc = xf - mu
    var = jnp.mean(xc * xc, axis=-1, keepdims=True)
    y = xc * lax.rsqrt(var + LN_EPS) * g.astype(jnp.float32) + b.astype(jnp.float32)
    return y.astype(x.dtype)


def multiscale_pool(ext_u, start):
    T = ext_u.shape[1] - POOL_HIST
    extf = ext_u.astype(jnp.float32)
    cs = jnp.pad(jnp.cumsum(extf, axis=1), ((0, 0), (1, 0), (0, 0)))
    hi = cs[:, POOL_HIST + 1:, :]
    pos = start + jnp.arange(T)
    outs = []
    for g, w in enumerate(POOL_WINDOWS):
        sl = slice(g * POOL_GROUP, (g + 1) * POOL_GROUP)
        lo = cs[:, POOL_HIST + 1 - w:POOL_HIST + 1 - w + T, sl]
        cnt = jnp.minimum(w, pos + 1).astype(jnp.float32)[None, :, None]
        outs.append((hi[:, :, sl] - lo) / cnt)
    mean = jnp.concatenate(outs, axis=-1)
    return (mean - extf[:, POOL_HIST:, :]).astype(ext_u.dtype)


def causal_depthwise(ext_v, w, b):
    out = lax.conv_general_dilated(
        ext_v, w[:, None, :].astype(ext_v.dtype), window_strides=(1,), padding='VALID',
        dimension_numbers=('NWC', 'WIO', 'NWC'), feature_group_count=CONV_WIDTH)
    return out + b


def mixer_layer(x, hist_pool, hist_conv, start, w_norm, w_in, w_pool_mix, pool_scale,
                w_pool_out, conv_w, conv_b, ln_g, ln_b, w_conv_out, w_out):
    B, T, _ = x.shape
    h = rmsnorm(x, w_norm)
    z = h @ w_in
    u, gp, cv, gc, mp, mc = jnp.split(z, SPLITS, axis=-1)
    ext_u = jnp.concatenate([hist_pool.astype(u.dtype), u], axis=1)
    pooled = multiscale_pool(ext_u, start)
    mixed = jnp.einsum('btgc,gcd->btgd', pooled.reshape(B, T, N_POOL_GROUPS, POOL_GROUP),
                       w_pool_mix).reshape(B, T, POOL_WIDTH) * pool_scale
    pool_branch = (mixed * jax.nn.silu(gp)) @ w_pool_out
    a, bg = jnp.split(cv, 2, axis=-1)
    v = a * jax.nn.sigmoid(bg)
    ext_v = jnp.concatenate([hist_conv.astype(v.dtype), v], axis=1)
    c = jax.nn.silu(layernorm(causal_depthwise(ext_v, conv_w, conv_b), ln_g, ln_b))
    conv_branch = (c * jax.nn.silu(gc)) @ w_conv_out
    merged = jax.nn.sigmoid(mp) * pool_branch + jax.nn.sigmoid(mc) * conv_branch
    y = x + merged @ w_out
    return y, ext_u[:, -POOL_HIST:, :], ext_v[:, -CONV_HIST:, :]


def setup_inputs(seed: int = 0) -> dict:
    key = jax.random.key(seed)
    ks = jax.random.split(key, 16)
    f32 = jnp.float32
    nrm = lambda k, s, sc: jax.random.normal(k, s, f32) * sc
    return {
        "x_prompt": nrm(ks[0], (BATCH, SEQ, D_MODEL), 1.0),
        "x_sample": nrm(ks[1], (DEC_BATCH, DEC_SEQ, D_MODEL), 1.0),
        "cache_pool": nrm(ks[2], (DEPTH, DEC_BATCH, POOL_HIST, POOL_WIDTH), 1.0),
        "cache_conv": nrm(ks[3], (DEPTH, DEC_BATCH, CONV_HIST, CONV_WIDTH), 0.5),
        "w_norm": 1.0 + nrm(ks[4], (DEPTH, D_MODEL), 0.02),
        "w_in": nrm(ks[5], (DEPTH, D_MODEL, IN_COLS), D_MODEL ** -0.5),
        "w_pool_mix": nrm(ks[6], (DEPTH, N_POOL_GROUPS, POOL_GROUP, POOL_GROUP), POOL_GROUP ** -0.5),
        "pool_scale": 1.0 + nrm(ks[7], (DEPTH, POOL_WIDTH), 0.1),
        "w_pool_out": nrm(ks[8], (DEPTH, POOL_WIDTH, D_MODEL), POOL_WIDTH ** -0.5),
        "conv_w": nrm(ks[9], (DEPTH, CONV_K, CONV_WIDTH), CONV_K ** -0.5),
        "conv_b": nrm(ks[10], (DEPTH, CONV_WIDTH), 0.02),
        "ln_g": 1.0 + nrm(ks[11], (DEPTH, CONV_WIDTH), 0.02),
        "ln_b": nrm(ks[12], (DEPTH, CONV_WIDTH), 0.02),
        "w_conv_out": nrm(ks[13], (DEPTH, CONV_WIDTH, D_MODEL), CONV_WIDTH ** -0.5),
        "w_out": nrm(ks[14], (DEPTH, D_MODEL, D_MODEL), D_MODEL ** -0.5),
        "w_final_norm": 1.0 + nrm(ks[15], (D_MODEL,), 0.02),
    }


def reference(x_prompt, x_sample, cache_pool, cache_conv, w_norm, w_in, w_pool_mix, pool_scale,
              w_pool_out, conv_w, conv_b, ln_g, ln_b, w_conv_out, w_out, w_final_norm):
    B = x_prompt.shape[0]
    hp = x_prompt
    hs = x_sample
    pool_p, conv_p, pool_s, conv_s = [], [], [], []
    zero_pool = jnp.zeros((B, POOL_HIST, POOL_WIDTH), x_prompt.dtype)
    zero_conv = jnp.zeros((B, CONV_HIST, CONV_WIDTH), x_prompt.dtype)
    for l in range(DEPTH):
        params = (w_norm[l], w_in[l], w_pool_mix[l], pool_scale[l], w_pool_out[l],
                  conv_w[l], conv_b[l], ln_g[l], ln_b[l], w_conv_out[l], w_out[l])
        hp, sp, sc = mixer_layer(hp, zero_pool, zero_conv, 0, *params)
        hs, tp, tc = mixer_layer(hs, cache_pool[l], cache_conv[l], PAST_LEN, *params)
        pool_p.append(sp)
        conv_p.append(sc)
        pool_s.append(tp)
        conv_s.append(tc)
    y_prompt = rmsnorm(hp, w_final_norm)
    y_sample = rmsnorm(hs, w_final_norm)
    new_pool_state_prompt = jnp.stack(pool_p, axis=0)
    new_conv_state_prompt = jnp.stack(conv_p, axis=0)
    new_pool_state_sample = jnp.stack(pool_s, axis=0)
    new_conv_state_sample = jnp.stack(conv_s, axis=0)
    return (y_prompt, y_sample, new_pool_state_prompt, new_conv_state_prompt,
            new_pool_state_sample, new_conv_state_sample)
```

```python
import numpy as np
import concourse.bass as bass
import concourse.mybir as mybir
from concourse.bass_utils import run_bass_kernel_spmd

F32 = mybir.dt.float32
BF16 = mybir.dt.bfloat16
AF = mybir.ActivationFunctionType
ALU = mybir.AluOpType

NCORES = 8
D = 2048
T = 576
TS = 64
TP = 512
EW = 640
ES0, EP0 = 32, 128
NKD = 16
NC8 = 8
RMS_EPS = 1e-6
LN_EPS = 1e-5
WINS = (2, 4, 8, 16)

LP = 16 + 8 + 8 + 8 + 8 + 8 * 31
O_WN, O_PS, O_CB, O_LG, O_LB, O_CW = 0, 16, 24, 32, 40, 48
O_WF = 2 * LP
O_FIX = O_WF + 16
NP_ = O_FIX + 2 * 4 * 16

NSLAB = 5
NTMP = 10
DEBUG_STOP = False

ZBASE = [0, 1024, 2048, 3072]
NZ = 4
NPAIR1 = 12


def ztiles(zb):
    base = ZBASE[zb]
    return [(0, 512, base), (512, 576, base + 512)]


class Sched:
    ENGS = ("pe", "act", "dve", "pool", "sp")

    def __init__(self):
        self.ops = {e: [] for e in self.ENGS}
        self.sem = {}
        self.cnt = {}
        self.state = {}
        self.known = {e: {} for e in self.ENGS}
        self.dma_tokens = []

    def _st(self, k):
        if k not in self.state:
            self.state[k] = {"W": [], "R": []}
        return self.state[k]

    def emit(self, eng, fn, reads=(), writes=(), extra=(), semname=None, inc=1, noself=False):
        waits = {}

        def add(tok):
            if tok is None:
                return
            s, v = tok
            if waits.get(s, 0) < v:
                waits[s] = v

        for k in reads:
            for t in self._st(k)["W"]:
                add(t)
        for k in writes:
            st = self._st(k)
            for t in st["W"]:
                add(t)
            for t in st["R"]:
                add(t)
        for t in extra:
            add(t)
        sname = semname if semname is not None else eng
        if inc > 1:
            add((sname, self.cnt.get(sname, 0)))
        wl = []
        kn = self.known[eng]
        for s, v in waits.items():
            if v <= 0:
                continue
            if noself and s == eng:
                continue
            if kn.get(s, 0) >= v:
                continue
            kn[s] = v
            wl.append((s, v))
        self.cnt[sname] = self.cnt.get(sname, 0) + inc
        tok = (sname, self.cnt[sname])
        self.ops[eng].append((wl, fn, sname, inc))
        for k in reads:
            self._st(k)["R"].append(tok)
        for k in writes:
            st = self._st(k)
            st["W"] = [tok]
            st["R"] = []
        return tok

    def run(self, eng, e, sems):
        for wl, fn, sname, inc in self.ops[eng]:
            for s, v in wl:
                e.wait_ge(sems[s], v)
            inst = fn(e)
            inst.then_inc(sems[sname], inc)


def build_program():
    nc = bass.Bass("TRN2", target_bir_lowering=False, dynamic_dma_scratch_size=4096)

    def din(name, shape):
        return nc.dram_tensor(name, shape, F32, kind="ExternalInput").ap()

    def dout(name, shape):
        return nc.dram_tensor(name, shape, F32, kind="ExternalOutput").ap()

    xT = din("xT", [2, NKD, 128, T])
    histu = din("histu", [2, 2, 128, NC8, 32])
    histv = din("histv", [2, 2, 128, NC8, 32])
    prm_d = din("prm", [128, NP_])
    w_in_r = din("w_in_r", [2, 72, 128, NKD, 128])
    w_po_r = din("w_po_r", [2, 16, 128, NC8, 128])
    w_co_r = din("w_co_r", [2, 16, 128, NC8, 128])
    w_out_r = din("w_out_r", [2, 16, 128, NKD, 128])
    w_mix_r = din("w_mix_r", [2, 128, 4, 2, 256])
    yT = dout("yT", [2, NKD, 128, T])
    ust = dout("ust", [2, 128, NC8, 2, 16])
    vst = dout("vst", [2, 128, NC8, 2, 32])

    S = Sched()
    sem_names = list(Sched.ENGS) + [f"slab{i}" for i in range(NSLAB)] + ["mix"] + [f"dq{i}" for i in range(8)]

    import contextlib
    with contextlib.ExitStack() as es:
        def sb(name, shape, dt):
            return es.enter_context(nc.sbuf_tensor(name, shape, dt))

        xres = sb("xres", [128, NKD, T], F32)
        hbf = sb("hbf", [128, NKD, T], BF16)
        extu = sb("extu", [128, NC8, EW], F32)
        extv = sb("extv", [128, NC8, EW], F32)
        pooled = sb("pooled", [128, NC8, T], BF16)
        poolin = sb("poolin", [128, NC8, T], BF16)
        ring = sb("ring", [128, NSLAB, NKD * 128], BF16)
        mixs = sb("mixs", [128, 4, 2, 256], BF16)
        prm = sb("prm_s", [128, NP_], F32)
        ones = sb("ones", [128, 128], BF16)
        ustate = sb("ustate", [128, 2, NC8, 2, 16], F32)
        vstate = sb("vstate", [128, 2, NC8, 2, 32], F32)
        tmps = sb("tmps", [128, NTMP, EW], F32)
        accsb = sb("accsb", [128, 2, T], F32)
        sgm = sb("sgm", [128, 16, T], BF16)
        ps = es.enter_context(nc.psum_tensor("ps", [128, 4096], F32))
        sems = {n: es.enter_context(nc.semaphore(n)) for n in sem_names}
        block = es.enter_context(nc.Block())

        convin = pooled
        merged_flat = extu[:].bitcast(BF16)

        def merged_ap(j, a=0, b=T):
            return merged_flat[:, j // 2, (j % 2) * EW + a:(j % 2) * EW + b]

        def mkey(j):
            return ("extu", j // 2)

        tstate = {"i": 0}

        tpin = set()
        zpin = set()

        def newtmp(pin=False):
            while True:
                i = tstate["i"] % (NTMP - 2)
                tstate["i"] += 1
                if i not in tpin:
                    break
            if pin:
                tpin.add(i)
            return i

        TR = NTMP - 2
        TM = NTMP - 1

        def tap(i, a=0, b=T):
            return tmps[:, i, a:b]

        tmps_bf = tmps[:].bitcast(BF16)

        def tapb(i, a=0, b=T):
            return tmps_bf[:, i, a:b]

        def zv(zb, a=0, b=T):
            return ps[:, ZBASE[zb] + a:ZBASE[zb] + b]

        zstate = {"i": 0}

        def newz(pin=False):
            assert len(zpin) < NZ
            while True:
                z = zstate["i"] % NZ
                zstate["i"] += 1
                if z not in zpin:
                    break
            if pin:
                zpin.add(z)
            return z

        dq = {"i": 0}

        def sp_dma(out, in_, reads=(), writes=(), out_dma=False):
            q = f"dq{dq['i'] % 8}"
            dq["i"] += 1
            tok = S.emit("sp", lambda e: e.dma_start(out=out, in_=in_), reads=reads, writes=writes,
                         semname=q, inc=16)
            if out_dma:
                S.dma_tokens.append(tok)
            return tok

        slab_state = {"n": 0, "free": [None] * NSLAB}

        def load_slab(src, nk):
            n = slab_state["n"]
            slot = n % NSLAB
            slab_state["n"] += 1
            extra = [slab_state["free"][slot]]
            if n < NSLAB and slab_state.get("gate") is not None:
                extra.append(slab_state["gate"])
            dst = ring[:, slot, 0:nk * 128]
            src2 = src.rearrange("p k n -> p (k n)")
            tok = S.emit("pool", lambda e: e.dma_start(out=dst, in_=src2), extra=extra,
                         semname=f"slab{slot}", inc=16)
            return slot, tok

        def slab_lhs(slot, k):
            return ring[:, slot, k * 128:(k + 1) * 128]

        def mm_group(zb, lhs_list, rhs_fn, reads, extra=(), slot=None):
            K = len(lhs_list)
            tiles = ztiles(zb)

            def fn(e):
                last = None
                for k in range(K):
                    for (a, b, pc) in tiles:
                        last = e.matmul(ps[:, pc:pc + (b - a)], lhs_list[k], rhs_fn(k, a, b),
                                        start=(k == 0), stop=(k == K - 1))
                return last

            tok = S.emit("pe", fn, reads=reads, writes=[("z", zb)], extra=extra)
            if slot is not None:
                slab_state["free"][slot] = tok
            return tok

        S.emit("dve", lambda e: e.memset(ones[:], 1.0), writes=[("ones",)])
        S.emit("dve", lambda e: e.memset(extu[:], 0.0), writes=[("extu", c) for c in range(NC8)])
        S.emit("dve", lambda e: e.memset(extv[:], 0.0), writes=[("extv", c) for c in range(NC8)])
        S.emit("dve", lambda e: e.memset(tmps[:], 0.0), writes=[("tmp", i) for i in range(NTMP)])
        sp_dma(prm[:], prm_d[:, :], writes=[("prm",)])
        S.emit("dve", lambda e: e.tensor_copy(tmps[:, 0, 0:1], prm[:, 0:1]), reads=[("prm",)], writes=[("tmp", 0)])
        S.emit("act", lambda e: e.copy(tmps[:, 1, 0:1], prm[:, 0:1]), reads=[("prm",)], writes=[("tmp", 1)])

        def pcol(off):
            return prm[:, off:off + 1]

        def sq_accum(z, c):
            t = newtmp()
            S.emit("act", lambda e: e.activation(tapb(t), xres[:, c, :], AF.Square),
                   reads=[("xres", c)], writes=[("tmp", t)])
            tiles = ztiles(z)

            def fn(e):
                last = None
                for (a, b, pc) in tiles:
                    last = e.matmul(ps[:, pc:pc + (b - a)], ones[:], tapb(t, a, b),
                                    start=(c == 0), stop=(c == NKD - 1))
                return last
            S.emit("pe", fn, reads=[("tmp", t), ("ones",)], writes=[("z", z)])

        def rms_stage(final, l, p, zpre=None):
            z = newz() if zpre is None else zpre
            for c in range(NKD if zpre is None else 0):
                t = newtmp()
                if c % 3 != 2:
                    S.emit("act", lambda e, t=t, c=c: e.activation(tapb(t), xres[:, c, :], AF.Square),
                           reads=[("xres", c)], writes=[("tmp", t)])
                else:
                    S.emit("dve", lambda e, t=t, c=c: e.tensor_tensor(tapb(t), xres[:, c, :], xres[:, c, :], ALU.mult),
                           reads=[("xres", c)], writes=[("tmp", t)])
                tiles = ztiles(z)

                def fn(e, t=t, c=c, tiles=tiles):
                    last = None
                    for (a, b, pc) in tiles:
                        last = e.matmul(ps[:, pc:pc + (b - a)], ones[:], tapb(t, a, b),
                                        start=(c == 0), stop=(c == NKD - 1))
                    return last
                S.emit("pe", fn, reads=[("tmp", t), ("ones",)], writes=[("z", z)])
            t = newtmp()
            S.emit("act", lambda e: e.activation(tap(t), zv(z), AF.Sqrt, bias=RMS_EPS, scale=1.0 / D),
                   reads=[("z", z)], writes=[("tmp", t)])
            S.emit("dve", lambda e: e.reciprocal(tap(TR), tap(t)), reads=[("tmp", t)], writes=[("tmp", TR)])
            zpin.discard(z)
            for c in range(NKD):
                if not final:
                    S.emit("dve", lambda e, c=c: e.scalar_tensor_tensor(
                        hbf[:, c, :], xres[:, c, :], pcol(l * LP + O_WN + c), tap(TR), ALU.mult, ALU.mult),
                        reads=[("xres", c), ("tmp", TR)], writes=[("h", c)])
                else:
                    t2 = newtmp()
                    S.emit("dve", lambda e, c=c, t2=t2: e.scalar_tensor_tensor(
                        tap(t2), xres[:, c, :], pcol(O_WF + c), tap(TR), ALU.mult, ALU.mult),
                        reads=[("xres", c), ("tmp", TR)], writes=[("tmp", t2)])
                    sp_dma(yT[p, c], tap(t2), reads=[("tmp", t2)], out_dma=True)

        sgm32 = sgm[:].rearrange("p j t -> p (j t)").bitcast(F32)

        def xB(c):
            if c < NC8:
                return extv[:, c, 64:64 + T]
            return sgm32[:, (c - NC8) * T:(c - NC8 + 1) * T]

        def xBkeys(c):
            if c < NC8:
                return [("extv", c)]
            return [("sgm", 2 * (c - NC8)), ("sgm", 2 * (c - NC8) + 1)]

        def stage_next_x_load():
            for q in range(8):
                src = xT[1, 2 * q:2 * q + 2].rearrange("c p t -> p c t")
                if q < 4:
                    dst = extv[:, 2 * q:2 * q + 2, 64:64 + T]
                else:
                    dst = sgm32[:, (2 * q - NC8) * T:(2 * q - NC8 + 2) * T].rearrange("p (c t) -> p c t", c=2)
                sp_dma(dst, src, writes=xBkeys(2 * q) + xBkeys(2 * q + 1))

        def sqB_accum(z, c):
            t = newtmp()
            S.emit("act", lambda e: e.activation(tapb(t), xB(c), AF.Square),
                   reads=xBkeys(c), writes=[("tmp", t)])
            tiles = ztiles(z)

            def fn(e):
                last = None
                for (a, b, pc) in tiles:
                    last = e.matmul(ps[:, pc:pc + (b - a)], ones[:], tapb(t, a, b),
                                    start=(c == 0), stop=(c == NKD - 1))
                return last
            S.emit("pe", fn, reads=[("tmp", t), ("ones",)], writes=[("z", z)])

        def hist_loads(p, l):
            sp_dma(extu[:, :, 0:32], histu[p, l], writes=[("extu", c) for c in range(NC8)])
            sp_dma(extv[:, :, 0:32], histv[p, l], writes=[("extv", c) for c in range(NC8)])

        def pass_layer(p, l, zpre):
            PB = l * LP
            if zpre != "HREADY":
                hist_loads(p, l)
            if zpre != "HREADY":
                rms_stage(False, l, p, zpre)

            first = {"v": True}

            def win_group(m, zb):
                slot, tk = load_slab(w_in_r[l, m], NKD)
                if first["v"]:
                    first["v"] = False
                    tiles = ztiles(zb)
                    tok = None
                    for k in range(NKD):
                        def fn(e, k=k):
                            last = None
                            for (a, b, pc) in tiles:
                                last = e.matmul(ps[:, pc:pc + (b - a)], slab_lhs(slot, k), hbf[:, k, a:b],
                                                start=(k == 0), stop=(k == NKD - 1))
                            return last
                        tok = S.emit("pe", fn, reads=[("h", k)], writes=[("z", zb)], extra=[tk])
                    slab_state["free"][slot] = tok
                    return tok
                return mm_group(zb, [slab_lhs(slot, k) for k in range(NKD)],
                                lambda k, a, b: hbf[:, k, a:b],
                                reads=[("h", k) for k in range(NKD)], extra=[tk], slot=slot)

            def vfront_pe(c):
                za = newz(pin=True)
                win_group(16 + c, za)
                zb = newz()
                win_group(24 + c, zb)
                t = newtmp(pin=True)
                S.emit("act", lambda e, t=t, zb=zb: e.activation(tap(t), zv(zb), AF.Sigmoid),
                       reads=[("z", zb)], writes=[("tmp", t)])
                return (za, t)

            def vfront_dve(c, st):
                za, t = st
                S.emit("dve", lambda e: e.tensor_tensor(
                    extv[:, c, ES0:ES0 + TS], zv(za, 0, TS), tap(t, 0, TS), ALU.mult),
                    reads=[("z", za), ("tmp", t)], writes=[("extv", c)])
                S.emit("dve", lambda e: e.tensor_tensor(
                    extv[:, c, EP0:EP0 + TP], zv(za, TS, T), tap(t, TS, T), ALU.mult),
                    reads=[("z", za), ("tmp", t)], writes=[("extv", c)])
                if p == 0:
                    S.emit("act", lambda e: e.copy(extv[:, c, 96:128], extv[:, c, 64:96]),
                           reads=[], writes=[("extv", c)])
                else:
                    S.emit("act", lambda e: e.copy(extv[:, c, 96:128], vstate[:, l, c, 1, :]),
                           reads=[("vstate", l)], writes=[("extv", c)])
                S.emit("act", lambda e: e.copy(vstate[:, l, c, 0, :], extv[:, c, 64:96]),
                       reads=[("extv", c)], writes=[("vstate", l)])
                S.emit("act", lambda e: e.copy(vstate[:, l, c, 1, :], extv[:, c, 608:640]),
                       reads=[("extv", c)], writes=[("vstate", l)])
                zpin.discard(za)
                tpin.discard(t)

            def conv_taps(c, k0, k1):
                for k in range(k0, k1):
                    wk = pcol(PB + O_CW + c * 31 + k)
                    for (key, a0, n, e0) in ((("accP", c % 2), TS, TP, EP0 - 30), (("accS", c % 2), 0, TS, ES0 - 30)):
                        accap = accsb[:, c % 2, a0:a0 + n]
                        src = extv[:, c, e0 + k:e0 + k + n]
                        if k == 0:
                            S.emit("dve", lambda e, accap=accap, src=src, wk=wk: e.tensor_scalar(
                                accap, src, wk, None, ALU.mult),
                                reads=[("extv", c)], writes=[key])
                        else:
                            S.emit("dve", lambda e, accap=accap, src=src, wk=wk: e.scalar_tensor_tensor(
                                accap, src, wk, accap, ALU.mult, ALU.add),
                                reads=[("extv", c)], writes=[key])

            def v_evac(c):
                S.emit("act", lambda e: e.activation(
                    extv[:, c, 0:T], accsb[:, c % 2, 0:T], AF.Identity, bias=pcol(PB + O_CB + c), scale=1.0),
                    reads=[("accP", c % 2), ("accS", c % 2)], writes=[("extv", c)])

            def u_pe(c):
                z = newz()
                win_group(c, z)
                S.emit("act", lambda e: e.copy(extu[:, c, ES0:ES0 + TS], zv(z, 0, TS)),
                       reads=[("z", z)], writes=[("extu", c)])
                S.emit("act", lambda e: e.copy(extu[:, c, EP0:EP0 + TP], zv(z, TS, T)),
                       reads=[("z", z)], writes=[("extu", c)])
                if p == 0:
                    S.emit("act", lambda e: e.copy(extu[:, c, 112:128], extu[:, c, 80:96]),
                           reads=[], writes=[("extu", c)])
                else:
                    S.emit("act", lambda e: e.copy(extu[:, c, 112:128], ustate[:, l, c, 1, :]),
                           reads=[("ustate", l)], writes=[("extu", c)])
                S.emit("act", lambda e: e.copy(ustate[:, l, c, 0, :], extu[:, c, 80:96]),
                       reads=[("extu", c)], writes=[("ustate", l)])
                S.emit("act", lambda e: e.copy(ustate[:, l, c, 1, :], extu[:, c, 624:640]),
                       reads=[("extu", c)], writes=[("ustate", l)])
                return None

            def u_dve(c, st):
                g = c // 2
                w = WINS[g]
                cur_key = ("extu", c)
                cur = lambda a, b: extu[:, c, a:b]
                sh = 1
                while sh < w:
                    t = newtmp()
                    S.emit("dve", lambda e, t=t, cur=cur, sh=sh: e.tensor_tensor(
                        tmps[:, t, 16:EW], cur(16, EW), cur(16 - sh, EW - sh), ALU.add),
                        reads=[cur_key], writes=[("tmp", t)])
                    cur_key = ("tmp", t)
                    cur = lambda a, b, t=t: tmps[:, t, a:b]
                    sh *= 2
                for (o0, e0, n) in ((0, ES0, TS), (TS, EP0, TP)):
                    S.emit("dve", lambda e, cur=cur, o0=o0, e0=e0, n=n: e.scalar_tensor_tensor(
                        pooled[:, c, o0:o0 + n], cur(e0, e0 + n), 1.0 / w, extu[:, c, e0:e0 + n],
                        ALU.mult, ALU.subtract),
                        reads=[cur_key, ("extu", c)], writes=[("pooled", c)])
                t = newtmp()
                fo = O_FIX + (p * 4 + g) * 16
                S.emit("dve", lambda e, t=t, cur=cur: e.tensor_tensor(
                    tmps[:, t, 0:16], cur(EP0, EP0 + 16), prm[:, fo:fo + 16], ALU.mult),
                    reads=[cur_key], writes=[("tmp", t)])
                S.emit("dve", lambda e, t=t: e.tensor_tensor(
                    pooled[:, c, TS:TS + 16], tmps[:, t, 0:16], extu[:, c, EP0:EP0 + 16], ALU.subtract),
                    reads=[("tmp", t), ("extu", c)], writes=[("pooled", c)])

            mixsrc = w_mix_r[l].rearrange("p g k n -> p (g k n)")
            mixdst = mixs[:].rearrange("p g k n -> p (g k n)")
            S.emit("pool", lambda e: e.dma_start(out=mixdst, in_=mixsrc), writes=[("mixs",)],
                   semname="mix", inc=16)

            def g_pe(c):
                g = c // 2
                half = c % 2
                zg = newz()
                win_group(8 + c, zg)
                t = newtmp(pin=True)
                S.emit("act", lambda e: e.activation(tap(t), zv(zg), AF.Silu),
                       reads=[("z", zg)], writes=[("tmp", t)])
                zm = newz(pin=True)
                mm_group(zm, [mixs[:, g, kk, half * 128:(half + 1) * 128] for kk in range(2)],
                         lambda k, a, b: pooled[:, 2 * g + k, a:b],
                         reads=[("pooled", 2 * g), ("pooled", 2 * g + 1), ("mixs",)])
                return (t, zm)

            def g_dve(c, st):
                t, zm = st
                S.emit("dve", lambda e: e.scalar_tensor_tensor(
                    poolin[:, c, :], zv(zm), pcol(PB + O_PS + c), tap(t), ALU.mult, ALU.mult),
                    reads=[("z", zm), ("tmp", t)], writes=[("poolin", c)])
                zpin.discard(zm)
                tpin.discard(t)

            def pair_pe(j):
                zmp = newz()
                win_group(40 + j, zmp)
                s1 = newtmp(pin=True)
                S.emit("act", lambda e: e.activation(tap(s1), zv(zmp), AF.Sigmoid),
                       reads=[("z", zmp)], writes=[("tmp", s1)])
                zpb = newz(pin=True)
                slot, tk = load_slab(w_po_r[l, j], NC8)
                mm_group(zpb, [slab_lhs(slot, k) for k in range(NC8)], lambda k, a, b: poolin[:, k, a:b],
                         reads=[("poolin", k) for k in range(NC8)], extra=[tk], slot=slot)
                return (s1, zpb)

            def pair_dve(j, st):
                s1, zpb = st
                S.emit("dve", lambda e: e.tensor_tensor(merged_ap(j), zv(zpb), tap(s1), ALU.mult),
                       reads=[("z", zpb), ("tmp", s1)], writes=[mkey(j), ("merged", j)])
                zpin.discard(zpb)
                tpin.discard(s1)

            def c_pe(c):
                zc = newz()
                win_group(32 + c, zc)
                S.emit("act", lambda e: e.activation(pooled[:, c, :], zv(zc), AF.Silu),
                       reads=[("z", zc)], writes=[("pooled", c)])
                return None

            def c_dve(c, st):
                return None

            def m_pe(j):
                zmc = newz()
                win_group(56 + j, zmc)
                S.emit("act", lambda e: e.activation(sgm[:, j, :], zv(zmc), AF.Sigmoid),
                       reads=[("z", zmc)], writes=[("sgm", j)])
                return None

            items = [("u", c) for c in range(NC8)] + [("g", c) for c in range(NC8)] + \
                    [("p", j) for j in range(NPAIR1)] + [("c", c) for c in range(NC8)] + \
                    [("m", j) for j in range(16)]
            sched = [[("u", 0), ("m", 0), ("u", 1), ("m", 1), ("u", 2), ("m", 2), ("u", 3), ("m", 3),
                      ("u", 4), ("m", 4), ("m", 5)],
                     [("u", 5), ("m", 6), ("u", 6), ("m", 7), ("u", 7), ("m", 8), ("g", 0), ("g", 1), ("g", 2)],
                     [("g", 3), ("g", 4), ("g", 5), ("g", 6), ("g", 7)],
                     [("p", 0), ("c", 0), ("p", 1), ("c", 1), ("m", 9)],
                     [("p", 2), ("c", 2), ("p", 3), ("c", 3), ("m", 10)],
                     [("p", 4), ("c", 4), ("p", 5), ("c", 5), ("m", 11)],
                     [("p", 6), ("c", 6), ("p", 7), ("c", 7), ("m", 12)],
                     [("p", 8), ("m", 13), ("p", 9), ("m", 14), ("p", 10), ("m", 15), ("p", 11)]]
            assert sorted(sum(sched, [])) == sorted(items)
            PEF = {"u": u_pe, "g": g_pe, "p": pair_pe, "c": c_pe, "m": m_pe}
            DVF = {"u": u_dve, "g": g_dve, "p": pair_dve, "c": c_dve, "m": c_dve}

            st0 = vfront_pe(0)
            vfront_dve(0, st0)
            for c in range(NC8):
                F = sched[c]
                n = len(F)
                sts = [None] * n
                bounds = [round(31 * i / (n + 1)) for i in range(n + 2)]
                if c > 0:
                    v_evac(c - 1)
                sts[0] = PEF[F[0][0]](F[0][1])
                stn = None
                for i in range(n + 1):
                    conv_taps(c, bounds[i], bounds[i + 1])
                    if i >= 1:
                        DVF[F[i - 1][0]](F[i - 1][1], sts[i - 1])
                    if i + 1 < n:
                        sts[i + 1] = PEF[F[i + 1][0]](F[i + 1][1])
                    if i + 1 == max(1, (2 * n) // 3) and c + 1 < NC8:
                        stn = vfront_pe(c + 1)
                if stn is not None:
                    vfront_dve(c + 1, stn)
            v_evac(NC8 - 1)
            if p == 1:
                sp_dma(ust[l], ustate[:, l], reads=[("ustate", l)], out_dma=True)
                sp_dma(vst[l], vstate[:, l], reads=[("vstate", l)], out_dma=True)

            assert NPAIR1 == 12
            sa = pair_pe(12)
            sb_ = pair_pe(13)
            pair_dve(12, sa)
            pair_dve(13, sb_)
            sa = pair_pe(14)
            sb_ = pair_pe(15)

            z1 = newz()
            z2 = newz()
            for c in range(NC8):
                ta = newtmp()
                S.emit("act", lambda e, ta=ta, c=c: e.copy(tapb(ta), extv[:, c, 0:T]),
                       reads=[("extv", c)], writes=[("tmp", ta)])
                tb = newtmp()
                S.emit("dve", lambda e, tb=tb, c=c: e.tensor_tensor(tapb(tb), extv[:, c, 0:T], extv[:, c, 0:T], ALU.mult),
                       reads=[("extv", c)], writes=[("tmp", tb)])
                for (zz, tt) in ((z1, ta), (z2, tb)):
                    tiles = ztiles(zz)

                    def fn(e, tt=tt, c=c, tiles=tiles):
                        last = None
                        for (a, b, pc) in tiles:
                            last = e.matmul(ps[:, pc:pc + (b - a)], ones[:], tapb(tt, a, b),
                                            start=(c == 0), stop=(c == NC8 - 1))
                        return last
                    S.emit("pe", fn, reads=[("tmp", tt), ("ones",)], writes=[("z", zz)])
            pair_dve(14, sa)
            pair_dve(15, sb_)
            S.emit("act", lambda e: e.activation(tap(TM), zv(z1), AF.Copy, scale=1.0 / 1024),
                   reads=[("z", z1)], writes=[("tmp", TM)])
            tq = newtmp()
            S.emit("act", lambda e: e.activation(tap(tq), zv(z1), AF.Square, scale=1.0 / 1024),
                   reads=[("z", z1)], writes=[("tmp", tq)])
            tv = newtmp()
            S.emit("dve", lambda e: e.scalar_tensor_tensor(
                tap(tv), zv(z2), 1.0 / 1024, tap(tq), ALU.mult, ALU.subtract),
                reads=[("z", z2), ("tmp", tq)], writes=[("tmp", tv)])
            tsd = newtmp()
            S.emit("act", lambda e: e.activation(tap(tsd), tap(tv), AF.Sqrt, bias=LN_EPS, scale=1.0),
                   reads=[("tmp", tv)], writes=[("tmp", tsd)])
            S.emit("dve", lambda e: e.reciprocal(tap(TR), tap(tsd)), reads=[("tmp", tsd)], writes=[("tmp", TR)])

            def gc_a(c):
                t1 = newtmp(pin=True)
                S.emit("dve", lambda e: e.tensor_tensor(tap(t1), extv[:, c, 0:T], tap(TM), ALU.subtract),
                       reads=[("extv", c), ("tmp", TM)], writes=[("tmp", t1)])
                S.emit("dve", lambda e: e.tensor_tensor(tap(t1), tap(t1), tap(TR), ALU.mult),
                       reads=[("tmp", TR)], writes=[("tmp", t1)])
                S.emit("act", lambda e: e.activation(
                    tap(t1), tap(t1), AF.Silu, bias=pcol(PB + O_LB + c), scale=pcol(PB + O_LG + c)),
                    reads=[], writes=[("tmp", t1)])
                return t1

            def gc_b(c, t1):
                S.emit("dve", lambda e: e.tensor_tensor(convin[:, c, :], tap(t1), convin[:, c, :], ALU.mult),
                       reads=[("tmp", t1)], writes=[("pooled", c)])
                tpin.discard(t1)

            NCB = 4
            cbz = [newz(pin=True) for _ in range(NCB)]
            cbs = [load_slab(w_co_r[l, j], NC8) for j in range(NCB)]
            cbtok = [None]

            def cb_step(k):
                def fn(e):
                    last = None
                    for j in range(NCB):
                        for (a, b, pc) in ztiles(cbz[j]):
                            last = e.matmul(ps[:, pc:pc + (b - a)], slab_lhs(cbs[j][0], k), convin[:, k, a:b],
                                            start=(k == 0), stop=(k == NC8 - 1))
                    return last
                cbtok[0] = S.emit("pe", fn, reads=[("pooled", k)], writes=[("z", z) for z in cbz],
                                  extra=[tk for (_, tk) in cbs])

            def cb_finish(j, zcb):
                s2 = newtmp()
                S.emit("dve", lambda e: e.tensor_tensor(tap(s2), zv(zcb), sgm[:, j, :], ALU.mult),
                       reads=[("z", zcb), ("sgm", j)], writes=[("tmp", s2)])
                S.emit("dve", lambda e: e.tensor_tensor(merged_ap(j), merged_ap(j), tap(s2), ALU.add),
                       reads=[("tmp", s2)], writes=[mkey(j), ("merged", j)])

            prev = None
            for c in range(NC8):
                cur_t = gc_a(c)
                if prev is not None:
                    gc_b(c - 1, prev)
                    cb_step(c - 1)
                prev = cur_t
            gc_b(NC8 - 1, prev)
            cb_step(NC8 - 1)
            for j in range(NCB):
                slab_state["free"][cbs[j][0]] = cbtok[0]
                cb_finish(j, cbz[j])
                zpin.discard(cbz[j])

            for j in range(NCB, 16):
                zcb = newz()
                slot, tk = load_slab(w_co_r[l, j], NC8)
                mm_group(zcb, [slab_lhs(slot, k) for k in range(NC8)], lambda k, a, b: convin[:, k, a:b],
                         reads=[("pooled", k) for k in range(NC8)], extra=[tk], slot=slot)
                cb_finish(j, zcb)

            stage_b = (p == 0 and l == 1 and not DEBUG_STOP)
            zs = newz(pin=True)
            zsB = None
            if stage_b:
                stage_next_x_load()
                zsB = newz(pin=True)
            for j in range(16):
                zo = newz()
                slot, tk = load_slab(w_out_r[l, j], NKD)
                mm_group(zo, [slab_lhs(slot, k) for k in range(NKD)], lambda k, a, b: merged_ap(k, a, b),
                         reads=[("merged", k) for k in range(NKD)] + [mkey(k) for k in range(0, NKD, 2)],
                         extra=[tk], slot=slot)
                S.emit("dve", lambda e, zo=zo, j=j: e.tensor_tensor(xres[:, j, :], xres[:, j, :], zv(zo), ALU.add),
                       reads=[("z", zo)], writes=[("xres", j)])
                if j >= 1:
                    sq_accum(zs, j - 1)
                if stage_b:
                    sqB_accum(zsB, j)
            sq_accum(zs, 15)
            if stage_b:
                return (zs, zsB)
            return zs

        for q in range(8):
            gtok = sp_dma(xres[:, 2 * q:2 * q + 2, :], xT[0, 2 * q:2 * q + 2].rearrange("c p t -> p c t"),
                          writes=[("xres", c) for c in range(2 * q, 2 * q + 2)])
            if q == 5:
                slab_state["gate"] = gtok
        zpre = None
        for l in range(2):
            if DEBUG_STOP and l == 1:
                break
            zpre = pass_layer(0, l, zpre)
        if DEBUG_STOP:
            rms_stage(True, 0, 0, zpre)
        else:
            zsA, zsB = zpre
            hist_loads(1, 0)
            tB = newtmp()
            S.emit("act", lambda e: e.activation(tap(tB), zv(zsB), AF.Sqrt, bias=RMS_EPS, scale=1.0 / D),
                   reads=[("z", zsB)], writes=[("tmp", tB)])
            S.emit("dve", lambda e: e.reciprocal(tap(TM), tap(tB)), reads=[("tmp", tB)], writes=[("tmp", TM)])
            zpin.discard(zsB)
            for c in range(NKD):
                S.emit("dve", lambda e, c=c: e.scalar_tensor_tensor(
                    hbf[:, c, :], xB(c), pcol(O_WN + c), tap(TM), ALU.mult, ALU.mult),
                    reads=xBkeys(c) + [("tmp", TM)], writes=[("h", c)])
            rms_stage(True, 0, 0, zsA)
            for c in range(NKD):
                S.emit("act", lambda e, c=c: e.copy(xres[:, c, :], xB(c)),
                       reads=xBkeys(c), writes=[("xres", c)])
            zpre = "HREADY"
            for l in range(2):
                zpre = pass_layer(1, l, zpre)
            rms_stage(True, 0, 1, zpre)

        def final_fn(e):
            return e.nop()
        final_waits = list(S.dma_tokens)

        @block.sync
        def _(e):
            S.run("sp", e, sems)
            done = {}
            for s, v in final_waits:
                done[s] = max(done.get(s, 0), v)
            for s, v in done.items():
                e.wait_ge(sems[s], v)

        @block.gpsimd
        def _(e):
            S.run("pool", e, sems)

        @block.tensor
        def _(e):
            S.run("pe", e, sems)

        @block.scalar
        def _(e):
            S.run("act", e, sems)

        @block.vector
        def _(e):
            S.run("dve", e, sems)

    return nc


def _relayout_weights(w_in, w_pool_mix, w_pool_out, w_conv_out, w_out):
    w_in_r = np.ascontiguousarray(
        w_in.reshape(2, 16, 128, 72, 128).transpose(0, 3, 2, 1, 4))
    w_po_r = np.ascontiguousarray(
        w_pool_out.reshape(2, 8, 128, 16, 128).transpose(0, 3, 2, 1, 4))
    w_co_r = np.ascontiguousarray(
        w_conv_out.reshape(2, 8, 128, 16, 128).transpose(0, 3, 2, 1, 4))
    w_out_r = np.ascontiguousarray(
        w_out.reshape(2, 16, 128, 16, 128).transpose(0, 3, 2, 1, 4))
    w_mix_r = np.ascontiguousarray(
        w_pool_mix.reshape(2, 4, 2, 128, 256).transpose(0, 3, 1, 2, 4))
    return w_in_r, w_po_r, w_co_r, w_out_r, w_mix_r


def _colmajor(v, n):
    return v.reshape(n, 128).T


def kernel(x_prompt, x_sample, cache_pool, cache_conv, w_norm, w_in, w_pool_mix, pool_scale,
           w_pool_out, conv_w, conv_b, ln_g, ln_b, w_conv_out, w_out, w_final_norm):
    f32 = np.float32
    x_prompt = np.asarray(x_prompt, f32)
    x_sample = np.asarray(x_sample, f32)
    cache_pool = np.asarray(cache_pool, f32)
    cache_conv = np.asarray(cache_conv, f32)
    w_in_r, w_po_r, w_co_r, w_out_r, w_mix_r = _relayout_weights(
        np.asarray(w_in, f32), np.asarray(w_pool_mix, f32), np.asarray(w_pool_out, f32),
        np.asarray(w_conv_out, f32), np.asarray(w_out, f32))

    prm_base = np.zeros((128, NP_), f32)
    for l in range(2):
        b = l * LP
        prm_base[:, b + O_WN:b + O_WN + 16] = _colmajor(np.asarray(w_norm, f32)[l], 16)
        prm_base[:, b + O_PS:b + O_PS + 8] = _colmajor(np.asarray(pool_scale, f32)[l], 8)
        prm_base[:, b + O_CB:b + O_CB + 8] = _colmajor(np.asarray(conv_b, f32)[l], 8)
        prm_base[:, b + O_LG:b + O_LG + 8] = _colmajor(np.asarray(ln_g, f32)[l], 8)
        prm_base[:, b + O_LB:b + O_LB + 8] = _colmajor(np.asarray(ln_b, f32)[l], 8)
        cw = np.asarray(conv_w, f32)[l]
        prm_base[:, b + O_CW:b + O_CW + 248] = cw.reshape(31, 8, 128).transpose(2, 1, 0).reshape(128, 248)
    prm_base[:, O_WF:O_WF + 16] = _colmajor(np.asarray(w_final_norm, f32), 16)

    in_maps = []
    xp = x_prompt[0]
    for i in range(NCORES):
        xa = np.zeros((T, D), f32)
        if i > 0:
            xa[0:TS] = xp[1024 * i - TS:1024 * i]
        xa[TS:] = xp[1024 * i:1024 * i + TP]
        xb = np.concatenate([x_sample[i], xp[1024 * i + TP:1024 * i + 2 * TP]], axis=0)
        xT = np.stack([xa.T.reshape(NKD, 128, T), xb.T.reshape(NKD, 128, T)], axis=0)
        hu = np.zeros((2, 2, 128, NC8, 32), f32)
        hv = np.zeros((2, 2, 128, NC8, 32), f32)
        for l in range(2):
            cu = cache_pool[l, i].T.reshape(NC8, 128, 15).transpose(1, 0, 2)
            hu[1, l, :, :, 17:32] = cu
            cv = cache_conv[l, i].T.reshape(NC8, 128, 30).transpose(1, 0, 2)
            hv[1, l, :, :, 2:32] = cv
        prm = prm_base.copy()
        for p in range(2):
            for g, w in enumerate(WINS):
                if i == 0 and p == 0:
                    cnt = np.minimum(w, np.arange(16) + 1).astype(f32)
                else:
                    cnt = np.full(16, w, f32)
                o = O_FIX + (p * 4 + g) * 16
                prm[:, o:o + 16] = (f32(1.0) / cnt)[None, :]
        in_maps.append({
            "xT": np.ascontiguousarray(xT), "histu": hu, "histv": hv, "prm": prm,
            "w_in_r": w_in_r, "w_po_r": w_po_r, "w_co_r": w_co_r, "w_out_r": w_out_r, "w_mix_r": w_mix_r,
        })

    nc = build_program()
    res = run_bass_kernel_spmd(nc, in_maps, core_ids=list(range(NCORES)))

    y_prompt = np.zeros((1, 8192, D), f32)
    y_sample = np.zeros((8, 64, D), f32)
    nps_p = np.zeros((2, 1, 15, 1024), f32)
    ncs_p = np.zeros((2, 1, 30, 1024), f32)
    nps_s = np.zeros((2, 8, 15, 1024), f32)
    ncs_s = np.zeros((2, 8, 30, 1024), f32)
    for i in range(NCORES):
        r = res.results[i]
        y = np.asarray(r["yT"], f32).reshape(2, D, T)
        y_prompt[0, 1024 * i:1024 * i + TP] = y[0][:, TS:].T
        y_prompt[0, 1024 * i + TP:1024 * i + 2 * TP] = y[1][:, TS:].T
        y_sample[i] = y[1][:, :TS].T
        us = np.asarray(r["ust"], f32)
        vs = np.asarray(r["vst"], f32)
        for l in range(2):
            nps_s[l, i] = us[l][:, :, 0, 1:16].transpose(2, 1, 0).reshape(15, 1024)
            ncs_s[l, i] = vs[l][:, :, 0, 2:32].transpose(2, 1, 0).reshape(30, 1024)
            if i == NCORES - 1:
                nps_p[l, 0] = us[l][:, :, 1, 1:16].transpose(2, 1, 0).reshape(15, 1024)
                ncs_p[l, 0] = vs[l][:, :, 1, 2:32].transpose(2, 1, 0).reshape(30, 1024)
    return (y_prompt, y_sample, nps_p, ncs_p, nps_s, ncs_s)
```

```python
import numpy as np
import concourse.bass as bass
import concourse.mybir as mybir
from concourse.bass_utils import run_bass_kernel_spmd

F32 = mybir.dt.float32
BF16 = mybir.dt.bfloat16
AF = mybir.ActivationFunctionType
ALU = mybir.AluOpType

NCORES = 8
D = 2048
T = 576
TS = 64
TP = 512
EW = 640
ES0, EP0 = 32, 128
NKD = 16
NC8 = 8
RMS_EPS = 1e-6
LN_EPS = 1e-5
WINS = (2, 4, 8, 16)

LP = 16 + 8 + 8 + 8 + 8 + 8 * 31
O_WN, O_PS, O_CB, O_LG, O_LB, O_CW = 0, 16, 24, 32, 40, 48
O_WF = 2 * LP
O_FIX = O_WF + 16
NP_ = O_FIX + 2 * 4 * 16

NSLAB = 5
NTMP = 10
DEBUG_STOP = False

ZBASE = [0, 1024, 2048, 3072]
NZ = 4
NM1 = 14
NPAIR1 = 12


def ztiles(zb):
    base = ZBASE[zb]
    return [(0, 512, base), (512, 576, base + 512)]


class Sched:
    ENGS = ("pe", "act", "dve", "pool", "sp")

    def __init__(self):
        self.ops = {e: [] for e in self.ENGS}
        self.sem = {}
        self.cnt = {}
        self.state = {}
        self.known = {e: {} for e in self.ENGS}
        self.dma_tokens = []

    def _st(self, k):
        if k not in self.state:
            self.state[k] = {"W": [], "R": []}
        return self.state[k]

    def emit(self, eng, fn, reads=(), writes=(), extra=(), semname=None, inc=1, noself=False):
        waits = {}

        def add(tok):
            if tok is None:
                return
            s, v = tok
            if waits.get(s, 0) < v:
                waits[s] = v

        for k in reads:
            for t in self._st(k)["W"]:
                add(t)
        for k in writes:
            st = self._st(k)
            for t in st["W"]:
                add(t)
            for t in st["R"]:
                add(t)
        for t in extra:
            add(t)
        sname = semname if semname is not None else eng
        if inc > 1:
            add((sname, self.cnt.get(sname, 0)))
        wl = []
        kn = self.known[eng]
        for s, v in waits.items():
            if v <= 0:
                continue
            if noself and s == eng:
                continue
            if kn.get(s, 0) >= v:
                continue
            kn[s] = v
            wl.append((s, v))
        self.cnt[sname] = self.cnt.get(sname, 0) + inc
        tok = (sname, self.cnt[sname])
        self.ops[eng].append((wl, fn, sname, inc))
        for k in reads:
            self._st(k)["R"].append(tok)
        for k in writes:
            st = self._st(k)
            st["W"] = [tok]
            st["R"] = []
        return tok

    def run(self, eng, e, sems):
        for wl, fn, sname, inc in self.ops[eng]:
            for s, v in wl:
                e.wait_ge(sems[s], v)
            inst = fn(e)
            inst.then_inc(sems[sname], inc)


def build_program():
    nc = bass.Bass("TRN2", target_bir_lowering=False, dynamic_dma_scratch_size=4096)

    def din(name, shape):
        return nc.dram_tensor(name, shape, F32, kind="ExternalInput").ap()

    def dout(name, shape):
        return nc.dram_tensor(name, shape, F32, kind="ExternalOutput").ap()

    xT = din("xT", [2, NKD, 128, T])
    histu = din("histu", [2, 2, 128, NC8, 32])
    histv = din("histv", [2, 2, 128, NC8, 32])
    prm_d = din("prm", [128, NP_])
    w_in_r = din("w_in_r", [2, 72, 128, NKD, 128])
    w_po_r = din("w_po_r", [2, 16, 128, NC8, 128])
    w_co_r = din("w_co_r", [2, 16, 128, NC8, 128])
    w_out_r = din("w_out_r", [2, 16, 128, NKD, 128])
    w_mix_r = din("w_mix_r", [2, 128, 4, 2, 256])
    yT = dout("yT", [2, NKD, 128, T])
    ust = dout("ust", [2, 128, NC8, 2, 16])
    vst = dout("vst", [2, 128, NC8, 2, 32])

    S = Sched()
    sem_names = list(Sched.ENGS) + [f"slab{i}" for i in range(NSLAB)] + ["mix"] + [f"dq{i}" for i in range(8)]

    import contextlib
    with contextlib.ExitStack() as es:
        def sb(name, shape, dt):
            return es.enter_context(nc.sbuf_tensor(name, shape, dt))

        xres = sb("xres", [128, NKD, T], F32)
        hbf = sb("hbf", [128, NKD, T], BF16)
        extu = sb("extu", [128, NC8, EW], F32)
        extv = sb("extv", [128, NC8, EW], F32)
        pooled = sb("pooled", [128, NC8, T], BF16)
        poolin = sb("poolin", [128, NC8, T], BF16)
        ring = sb("ring", [128, NSLAB, NKD * 128], BF16)
        mixs = sb("mixs", [128, 4, 2, 256], BF16)
        prm = sb("prm_s", [128, NP_], F32)
        ones = sb("ones", [128, 128], BF16)
        ustate = sb("ustate", [128, 2, NC8, 2, 16], F32)
        vstate = sb("vstate", [128, 2, NC8, 2, 32], F32)
        tmps = sb("tmps", [128, NTMP, EW], F32)
        accsb = sb("accsb", [128, 2, T], F32)
        sgm = sb("sgm", [128, 16, T], BF16)
        ps = es.enter_context(nc.psum_tensor("ps", [128, 4096], F32))
        sems = {n: es.enter_context(nc.semaphore(n)) for n in sem_names}
        block = es.enter_context(nc.Block())

        convin = pooled
        merged_flat = extu[:].bitcast(BF16)

        def merged_ap(j, a=0, b=T):
            return merged_flat[:, j // 2, (j % 2) * EW + a:(j % 2) * EW + b]

        def mkey(j):
            return ("extu", j // 2)

        tstate = {"i": 0}

        tpin = set()
        zpin = set()

        def newtmp(pin=False):
            while True:
                i = tstate["i"] % (NTMP - 2)
                tstate["i"] += 1
                if i not in tpin:
                    break
            if pin:
                tpin.add(i)
            return i

        TR = NTMP - 2
        TM = NTMP - 1

        def tap(i, a=0, b=T):
            return tmps[:, i, a:b]

        tmps_bf = tmps[:].bitcast(BF16)

        def tapb(i, a=0, b=T):
            return tmps_bf[:, i, a:b]

        def zv(zb, a=0, b=T):
            return ps[:, ZBASE[zb] + a:ZBASE[zb] + b]

        zstate = {"i": 0}

        def newz(pin=False):
            assert len(zpin) < NZ
            while True:
                z = zstate["i"] % NZ
                zstate["i"] += 1
                if z not in zpin:
                    break
            if pin:
                zpin.add(z)
            return z

        dq = {"i": 0}

        def sp_dma(out, in_, reads=(), writes=(), out_dma=False):
            q = f"dq{dq['i'] % 8}"
            dq["i"] += 1
            tok = S.emit("sp", lambda e: e.dma_start(out=out, in_=in_), reads=reads, writes=writes,
                         semname=q, inc=16)
            if out_dma:
                S.dma_tokens.append(tok)
            return tok

        slab_state = {"n": 0, "free": [None] * NSLAB}

        def load_slab(src, nk):
            n = slab_state["n"]
            slot = n % NSLAB
            slab_state["n"] += 1
            extra = [slab_state["free"][slot]]
            if n < NSLAB and slab_state.get("gate") is not None:
                extra.append(slab_state["gate"])
            dst = ring[:, slot, 0:nk * 128]
            src2 = src.rearrange("p k n -> p (k n)")
            tok = S.emit("pool", lambda e: e.dma_start(out=dst, in_=src2), extra=extra,
                         semname=f"slab{slot}", inc=16)
            return slot, tok

        def slab_lhs(slot, k):
            return ring[:, slot, k * 128:(k + 1) * 128]

        def mm_group(zb, lhs_list, rhs_fn, reads, extra=(), slot=None):
            K = len(lhs_list)
            tiles = ztiles(zb)

            def fn(e):
                last = None
                for k in range(K):
                    for (a, b, pc) in tiles:
                        last = e.matmul(ps[:, pc:pc + (b - a)], lhs_list[k], rhs_fn(k, a, b),
                                        start=(k == 0), stop=(k == K - 1))
                return last

            tok = S.emit("pe", fn, reads=reads, writes=[("z", zb)], extra=extra)
            if slot is not None:
                slab_state["free"][slot] = tok
            return tok

        S.emit("dve", lambda e: e.memset(ones[:], 1.0), writes=[("ones",)])
        S.emit("dve", lambda e: e.memset(extu[:], 0.0), writes=[("extu", c) for c in range(NC8)])
        S.emit("dve", lambda e: e.memset(extv[:], 0.0), writes=[("extv", c) for c in range(NC8)])
        S.emit("dve", lambda e: e.memset(tmps[:], 0.0), writes=[("tmp", i) for i in range(NTMP)])
        sp_dma(prm[:], prm_d[:, :], writes=[("prm",)])
        S.emit("dve", lambda e: e.tensor_copy(tmps[:, 0, 0:1], prm[:, 0:1]), reads=[("prm",)], writes=[("tmp", 0)])
        S.emit("act", lambda e: e.copy(tmps[:, 1, 0:1], prm[:, 0:1]), reads=[("prm",)], writes=[("tmp", 1)])

        def pcol(off):
            return prm[:, off:off + 1]

        def sq_accum(z, c):
            t = newtmp()
            S.emit("act", lambda e: e.activation(tapb(t), xres[:, c, :], AF.Square),
                   reads=[("xres", c)], writes=[("tmp", t)])
            tiles = ztiles(z)

            def fn(e):
                last = None
                for (a, b, pc) in tiles:
                    last = e.matmul(ps[:, pc:pc + (b - a)], ones[:], tapb(t, a, b),
                                    start=(c == 0), stop=(c == NKD - 1))
                return last
            S.emit("pe", fn, reads=[("tmp", t), ("ones",)], writes=[("z", z)])

        def rms_stage(final, l, p, zpre=None):
            z = newz() if zpre is None else zpre
            for c in range(NKD if zpre is None else 0):
                t = newtmp()
                if c % 3 != 2:
                    S.emit("act", lambda e, t=t, c=c: e.activation(tapb(t), xres[:, c, :], AF.Square),
                           reads=[("xres", c)], writes=[("tmp", t)])
                else:
                    S.emit("dve", lambda e, t=t, c=c: e.tensor_tensor(tapb(t), xres[:, c, :], xres[:, c, :], ALU.mult),
                           reads=[("xres", c)], writes=[("tmp", t)])
                tiles = ztiles(z)

                def fn(e, t=t, c=c, tiles=tiles):
                    last = None
                    for (a, b, pc) in tiles:
                        last = e.matmul(ps[:, pc:pc + (b - a)], ones[:], tapb(t, a, b),
                                        start=(c == 0), stop=(c == NKD - 1))
                    return last
                S.emit("pe", fn, reads=[("tmp", t), ("ones",)], writes=[("z", z)])
            t = newtmp()
            S.emit("act", lambda e: e.activation(tap(t), zv(z), AF.Sqrt, bias=RMS_EPS, scale=1.0 / D),
                   reads=[("z", z)], writes=[("tmp", t)])
            S.emit("dve", lambda e: e.reciprocal(tap(TR), tap(t)), reads=[("tmp", t)], writes=[("tmp", TR)])
            zpin.discard(z)
            for c in range(NKD):
                if not final:
                    S.emit("dve", lambda e, c=c: e.scalar_tensor_tensor(
                        hbf[:, c, :], xres[:, c, :], pcol(l * LP + O_WN + c), tap(TR), ALU.mult, ALU.mult),
                        reads=[("xres", c), ("tmp", TR)], writes=[("h", c)])
                else:
                    t2 = newtmp()
                    S.emit("dve", lambda e, c=c, t2=t2: e.scalar_tensor_tensor(
                        tap(t2), xres[:, c, :], pcol(O_WF + c), tap(TR), ALU.mult, ALU.mult),
                        reads=[("xres", c), ("tmp", TR)], writes=[("tmp", t2)])
                    sp_dma(yT[p, c], tap(t2), reads=[("tmp", t2)], out_dma=True)

        sgm32 = sgm[:].rearrange("p j t -> p (j t)").bitcast(F32)

        def xB(c):
            if c < NC8:
                return extv[:, c, 64:64 + T]
            return sgm32[:, (c - NC8) * T:(c - NC8 + 1) * T]

        def xBkeys(c):
            if c < NC8:
                return [("extv", c)]
            return [("sgm", 2 * (c - NC8)), ("sgm", 2 * (c - NC8) + 1)]

        def stage_next_x_load():
            for q in range(8):
                src = xT[1, 2 * q:2 * q + 2].rearrange("c p t -> p c t")
                if q < 4:
                    dst = extv[:, 2 * q:2 * q + 2, 64:64 + T]
                else:
                    dst = sgm32[:, (2 * q - NC8) * T:(2 * q - NC8 + 2) * T].rearrange("p (c t) -> p c t", c=2)
                sp_dma(dst, src, writes=xBkeys(2 * q) + xBkeys(2 * q + 1))

        def sqB_accum(z, c):
            t = newtmp()
            S.emit("act", lambda e: e.activation(tapb(t), xB(c), AF.Square),
                   reads=xBkeys(c), writes=[("tmp", t)])
            tiles = ztiles(z)

            def fn(e):
                last = None
                for (a, b, pc) in tiles:
                    last = e.matmul(ps[:, pc:pc + (b - a)], ones[:], tapb(t, a, b),
                                    start=(c == 0), stop=(c == NKD - 1))
                return last
            S.emit("pe", fn, reads=[("tmp", t), ("ones",)], writes=[("z", z)])

        def hist_loads(p, l):
            sp_dma(extu[:, :, 0:32], histu[p, l], writes=[("extu", c) for c in range(NC8)])
            sp_dma(extv[:, :, 0:32], histv[p, l], writes=[("extv", c) for c in range(NC8)])

        def pass_layer(p, l, zpre):
            PB = l * LP
            if zpre != "HREADY":
                hist_loads(p, l)
            if zpre != "HREADY":
                rms_stage(False, l, p, zpre)

            first = {"v": True}

            def win_group(m, zb):
                slot, tk = load_slab(w_in_r[l, m], NKD)
                if first["v"]:
                    first["v"] = False
                    tiles = ztiles(zb)
                    tok = None
                    for k in range(NKD):
                        def fn(e, k=k):
                            last = None
                            for (a, b, pc) in tiles:
                                last = e.matmul(ps[:, pc:pc + (b - a)], slab_lhs(slot, k), hbf[:, k, a:b],
                                                start=(k == 0), stop=(k == NKD - 1))
                            return last
                        tok = S.emit("pe", fn, reads=[("h", k)], writes=[("z", zb)], extra=[tk])
                    slab_state["free"][slot] = tok
                    return tok
                return mm_group(zb, [slab_lhs(slot, k) for k in range(NKD)],
                                lambda k, a, b: hbf[:, k, a:b],
                                reads=[("h", k) for k in range(NKD)], extra=[tk], slot=slot)

            def vfront_pe(c):
                za = newz(pin=True)
                win_group(16 + c, za)
                zb = newz()
                win_group(24 + c, zb)
                t = newtmp(pin=True)
                S.emit("act", lambda e, t=t, zb=zb: e.activation(tap(t), zv(zb), AF.Sigmoid),
                       reads=[("z", zb)], writes=[("tmp", t)])
                return (za, t)

            def vfront_dve(c, st):
                za, t = st
                S.emit("dve", lambda e: e.tensor_tensor(
                    extv[:, c, ES0:ES0 + TS], zv(za, 0, TS), tap(t, 0, TS), ALU.mult),
                    reads=[("z", za), ("tmp", t)], writes=[("extv", c)])
                S.emit("dve", lambda e: e.tensor_tensor(
                    extv[:, c, EP0:EP0 + TP], zv(za, TS, T), tap(t, TS, T), ALU.mult),
                    reads=[("z", za), ("tmp", t)], writes=[("extv", c)])
                if p == 0:
                    S.emit("act", lambda e: e.copy(extv[:, c, 96:128], extv[:, c, 64:96]),
                           reads=[], writes=[("extv", c)])
                else:
                    S.emit("act", lambda e: e.copy(extv[:, c, 96:128], vstate[:, l, c, 1, :]),
                           reads=[("vstate", l)], writes=[("extv", c)])
                S.emit("act", lambda e: e.copy(vstate[:, l, c, 0, :], extv[:, c, 64:96]),
                       reads=[("extv", c)], writes=[("vstate", l)])
                S.emit("act", lambda e: e.copy(vstate[:, l, c, 1, :], extv[:, c, 608:640]),
                       reads=[("extv", c)], writes=[("vstate", l)])
                zpin.discard(za)
                tpin.discard(t)

            def conv_taps(c, k0, k1):
                for k in range(k0, k1):
                    wk = pcol(PB + O_CW + c * 31 + k)
                    for (key, a0, n, e0) in ((("accP", c % 2), TS, TP, EP0 - 30), (("accS", c % 2), 0, TS, ES0 - 30)):
                        accap = accsb[:, c % 2, a0:a0 + n]
                        src = extv[:, c, e0 + k:e0 + k + n]
                        if k == 0:
                            S.emit("dve", lambda e, accap=accap, src=src, wk=wk: e.tensor_scalar(
                                accap, src, wk, None, ALU.mult),
                                reads=[("extv", c)], writes=[key])
                        else:
                            S.emit("dve", lambda e, accap=accap, src=src, wk=wk: e.scalar_tensor_tensor(
                                accap, src, wk, accap, ALU.mult, ALU.add),
                                reads=[("extv", c)], writes=[key])

            def v_evac(c):
                S.emit("act", lambda e: e.activation(
                    extv[:, c, 0:T], accsb[:, c % 2, 0:T], AF.Identity, bias=pcol(PB + O_CB + c), scale=1.0),
                    reads=[("accP", c % 2), ("accS", c % 2)], writes=[("extv", c)])

            def u_pe(c):
                z = newz()
                win_group(c, z)
                S.emit("act", lambda e: e.copy(extu[:, c, ES0:ES0 + TS], zv(z, 0, TS)),
                       reads=[("z", z)], writes=[("extu", c)])
                S.emit("act", lambda e: e.copy(extu[:, c, EP0:EP0 + TP], zv(z, TS, T)),
                       reads=[("z", z)], writes=[("extu", c)])
                if p == 0:
                    S.emit("act", lambda e: e.copy(extu[:, c, 112:128], extu[:, c, 80:96]),
                           reads=[], writes=[("extu", c)])
                else:
                    S.emit("act", lambda e: e.copy(extu[:, c, 112:128], ustate[:, l, c, 1, :]),
                           reads=[("ustate", l)], writes=[("extu", c)])
                S.emit("act", lambda e: e.copy(ustate[:, l, c, 0, :], extu[:, c, 80:96]),
                       reads=[("extu", c)], writes=[("ustate", l)])
                S.emit("act", lambda e: e.copy(ustate[:, l, c, 1, :], extu[:, c, 624:640]),
                       reads=[("extu", c)], writes=[("ustate", l)])
                return None

            def u_dve(c, st):
                g = c // 2
                w = WINS[g]
                cur_key = ("extu", c)
                cur = lambda a, b: extu[:, c, a:b]
                sh = 1
                while sh < w:
                    t = newtmp()
                    S.emit("dve", lambda e, t=t, cur=cur, sh=sh: e.tensor_tensor(
                        tmps[:, t, 16:EW], cur(16, EW), cur(16 - sh, EW - sh), ALU.add),
                        reads=[cur_key], writes=[("tmp", t)])
                    cur_key = ("tmp", t)
                    cur = lambda a, b, t=t: tmps[:, t, a:b]
                    sh *= 2
                for (o0, e0, n) in ((0, ES0, TS), (TS, EP0, TP)):
                    S.emit("dve", lambda e, cur=cur, o0=o0, e0=e0, n=n: e.scalar_tensor_tensor(
                        pooled[:, c, o0:o0 + n], cur(e0, e0 + n), 1.0 / w, extu[:, c, e0:e0 + n],
                        ALU.mult, ALU.subtract),
                        reads=[cur_key, ("extu", c)], writes=[("pooled", c)])
                t = newtmp()
                fo = O_FIX + (p * 4 + g) * 16
                S.emit("dve", lambda e, t=t, cur=cur: e.tensor_tensor(
                    tmps[:, t, 0:16], cur(EP0, EP0 + 16), prm[:, fo:fo + 16], ALU.mult),
                    reads=[cur_key], writes=[("tmp", t)])
                S.emit("dve", lambda e, t=t: e.tensor_tensor(
                    pooled[:, c, TS:TS + 16], tmps[:, t, 0:16], extu[:, c, EP0:EP0 + 16], ALU.subtract),
                    reads=[("tmp", t), ("extu", c)], writes=[("pooled", c)])

            mixsrc = w_mix_r[l].rearrange("p g k n -> p (g k n)")
            mixdst = mixs[:].rearrange("p g k n -> p (g k n)")
            S.emit("pool", lambda e: e.dma_start(out=mixdst, in_=mixsrc), writes=[("mixs",)],
                   semname="mix", inc=16)

            def g_pe(c):
                g = c // 2
                half = c % 2
                zg = newz()
                win_group(8 + c, zg)
                t = newtmp(pin=True)
                S.emit("act", lambda e: e.activation(tap(t), zv(zg), AF.Silu),
                       reads=[("z", zg)], writes=[("tmp", t)])
                zm = newz(pin=True)
                mm_group(zm, [mixs[:, g, kk, half * 128:(half + 1) * 128] for kk in range(2)],
                         lambda k, a, b: pooled[:, 2 * g + k, a:b],
                         reads=[("pooled", 2 * g), ("pooled", 2 * g + 1), ("mixs",)])
                return (t, zm)

            def g_dve(c, st):
                t, zm = st
                S.emit("dve", lambda e: e.scalar_tensor_tensor(
                    poolin[:, c, :], zv(zm), pcol(PB + O_PS + c), tap(t), ALU.mult, ALU.mult),
                    reads=[("z", zm), ("tmp", t)], writes=[("poolin", c)])
                zpin.discard(zm)
                tpin.discard(t)

            def pair_pe(j):
                zmp = newz()
                win_group(40 + j, zmp)
                s1 = newtmp(pin=True)
                S.emit("act", lambda e: e.activation(tap(s1), zv(zmp), AF.Sigmoid),
                       reads=[("z", zmp)], writes=[("tmp", s1)])
                zpb = newz(pin=True)
                slot, tk = load_slab(w_po_r[l, j], NC8)
                mm_group(zpb, [slab_lhs(slot, k) for k in range(NC8)], lambda k, a, b: poolin[:, k, a:b],
                         reads=[("poolin", k) for k in range(NC8)], extra=[tk], slot=slot)
                return (s1, zpb)

            def pair_dve(j, st):
                s1, zpb = st
                S.emit("dve", lambda e: e.tensor_tensor(merged_ap(j), zv(zpb), tap(s1), ALU.mult),
                       reads=[("z", zpb), ("tmp", s1)], writes=[mkey(j), ("merged", j)])
                zpin.discard(zpb)
                tpin.discard(s1)

            def c_pe(c):
                zc = newz()
                win_group(32 + c, zc)
                S.emit("act", lambda e: e.activation(pooled[:, c, :], zv(zc), AF.Silu),
                       reads=[("z", zc)], writes=[("pooled", c)])
                return None

            def c_dve(c, st):
                return None

            def m_pe(j):
                zmc = newz()
                win_group(56 + j, zmc)
                S.emit("act", lambda e: e.activation(sgm[:, j, :], zv(zmc), AF.Sigmoid),
                       reads=[("z", zmc)], writes=[("sgm", j)])
                return None

            items = [("u", c) for c in range(NC8)] + [("g", c) for c in range(NC8)] + \
                    [("p", j) for j in range(NPAIR1)] + [("c", c) for c in range(NC8)] + \
                    [("m", j) for j in range(NM1)]
            sched = [[("u", 0), ("m", 0), ("u", 1), ("m", 1), ("u", 2), ("m", 2), ("u", 3), ("m", 3),
                      ("u", 4), ("m", 4), ("m", 5)],
                     [("u", 5), ("m", 6), ("u", 6), ("m", 7), ("u", 7), ("m", 8), ("g", 0), ("g", 1), ("g", 2)],
                     [("g", 3), ("g", 4), ("g", 5), ("g", 6), ("g", 7)],
                     [("p", 0), ("c", 0), ("p", 1), ("c", 1), ("m", 9)],
                     [("p", 2), ("c", 2), ("p", 3), ("c", 3), ("m", 10)],
                     [("p", 4), ("c", 4), ("p", 5), ("c", 5), ("m", 11)],
                     [("p", 6), ("c", 6), ("p", 7), ("c", 7), ("m", 12)],
                     [("p", 8), ("m", 13), ("p", 9), ("p", 10), ("p", 11)]]
            assert sorted(sum(sched, [])) == sorted(items)
            PEF = {"u": u_pe, "g": g_pe, "p": pair_pe, "c": c_pe, "m": m_pe}
            DVF = {"u": u_dve, "g": g_dve, "p": pair_dve, "c": c_dve, "m": c_dve}

            st0 = vfront_pe(0)
            vfront_dve(0, st0)
            for c in range(NC8):
                F = sched[c]
                n = len(F)
                sts = [None] * n
                bounds = [round(31 * i / (n + 1)) for i in range(n + 2)]
                if c > 0:
                    v_evac(c - 1)
                sts[0] = PEF[F[0][0]](F[0][1])
                stn = None
                for i in range(n + 1):
                    conv_taps(c, bounds[i], bounds[i + 1])
                    if i >= 1:
                        DVF[F[i - 1][0]](F[i - 1][1], sts[i - 1])
                    if i + 1 < n:
                        sts[i + 1] = PEF[F[i + 1][0]](F[i + 1][1])
                    if i + 1 == max(1, (2 * n) // 3) and c + 1 < NC8:
                        stn = vfront_pe(c + 1)
                if stn is not None:
                    vfront_dve(c + 1, stn)
            v_evac(NC8 - 1)
            if p == 1:
                sp_dma(ust[l], ustate[:, l], reads=[("ustate", l)], out_dma=True)
                sp_dma(vst[l], vstate[:, l], reads=[("vstate", l)], out_dma=True)

            assert NPAIR1 == 12
            sa = pair_pe(12)
            sb_ = pair_pe(13)
            pair_dve(12, sa)
            pair_dve(13, sb_)
            sa = pair_pe(14)
            sb_ = pair_pe(15)

            z1 = newz()
            z2 = newz()
            for c in range(NC8):
                ta = newtmp()
                S.emit("act", lambda e, ta=ta, c=c: e.copy(tapb(ta), extv[:, c, 0:T]),
                       reads=[("extv", c)], writes=[("tmp", ta)])
                tb = newtmp()
                S.emit("dve", lambda e, tb=tb, c=c: e.tensor_tensor(tapb(tb), extv[:, c, 0:T], extv[:, c, 0:T], ALU.mult),
                       reads=[("extv", c)], writes=[("tmp", tb)])
                for (zz, tt) in ((z1, ta), (z2, tb)):
                    tiles = ztiles(zz)

                    def fn(e, tt=tt, c=c, tiles=tiles):
                        last = None
                        for (a, b, pc) in tiles:
                            last = e.matmul(ps[:, pc:pc + (b - a)], ones[:], tapb(tt, a, b),
                                            start=(c == 0), stop=(c == NC8 - 1))
                        return last
                    S.emit("pe", fn, reads=[("tmp", tt), ("ones",)], writes=[("z", zz)])
            pair_dve(14, sa)
            pair_dve(15, sb_)
            S.emit("act", lambda e: e.activation(tap(TM), zv(z1), AF.Copy, scale=1.0 / 1024),
                   reads=[("z", z1)], writes=[("tmp", TM)])
            tq = newtmp()
            S.emit("act", lambda e: e.activation(tap(tq), zv(z1), AF.Square, scale=1.0 / 1024),
                   reads=[("z", z1)], writes=[("tmp", tq)])
            tv = newtmp()
            S.emit("dve", lambda e: e.scalar_tensor_tensor(
                tap(tv), zv(z2), 1.0 / 1024, tap(tq), ALU.mult, ALU.subtract),
                reads=[("z", z2), ("tmp", tq)], writes=[("tmp", tv)])
            tsd = newtmp()
            S.emit("act", lambda e: e.activation(tap(tsd), tap(tv), AF.Sqrt, bias=LN_EPS, scale=1.0),
                   reads=[("tmp", tv)], writes=[("tmp", tsd)])
            S.emit("dve", lambda e: e.reciprocal(tap(TR), tap(tsd)), reads=[("tmp", tsd)], writes=[("tmp", TR)])
            for j in range(NM1, 16):
                m_pe(j)

            def gc_a(c):
                t1 = newtmp(pin=True)
                S.emit("dve", lambda e: e.tensor_tensor(tap(t1), extv[:, c, 0:T], tap(TM), ALU.subtract),
                       reads=[("extv", c), ("tmp", TM)], writes=[("tmp", t1)])
                S.emit("dve", lambda e: e.tensor_tensor(tap(t1), tap(t1), tap(TR), ALU.mult),
                       reads=[("tmp", TR)], writes=[("tmp", t1)])
                S.emit("act", lambda e: e.activation(
                    tap(t1), tap(t1), AF.Silu, bias=pcol(PB + O_LB + c), scale=pcol(PB + O_LG + c)),
                    reads=[], writes=[("tmp", t1)])
                return t1

            def gc_b(c, t1):
                S.emit("dve", lambda e: e.tensor_tensor(convin[:, c, :], tap(t1), convin[:, c, :], ALU.mult),
                       reads=[("tmp", t1)], writes=[("pooled", c)])
                tpin.discard(t1)

            NCB = 4
            cbz = [newz(pin=True) for _ in range(NCB)]
            cbs = [load_slab(w_co_r[l, j], NC8) for j in range(NCB)]
            cbtok = [None]

            def cb_step(k):
                def fn(e):
                    last = None
                    for j in range(NCB):
                        for (a, b, pc) in ztiles(cbz[j]):
                            last = e.matmul(ps[:, pc:pc + (b - a)], slab_lhs(cbs[j][0], k), convin[:, k, a:b],
                                            start=(k == 0), stop=(k == NC8 - 1))
                    return last
                cbtok[0] = S.emit("pe", fn, reads=[("pooled", k)], writes=[("z", z) for z in cbz],
                                  extra=[tk for (_, tk) in cbs])

            def cb_finish(j, zcb):
                s2 = newtmp()
                S.emit("dve", lambda e: e.tensor_tensor(tap(s2), zv(zcb), sgm[:, j, :], ALU.mult),
                       reads=[("z", zcb), ("sgm", j)], writes=[("tmp", s2)])
                S.emit("dve", lambda e: e.tensor_tensor(merged_ap(j), merged_ap(j), tap(s2), ALU.add),
                       reads=[("tmp", s2)], writes=[mkey(j), ("merged", j)])

            prev = None
            for c in range(NC8):
                cur_t = gc_a(c)
                if prev is not None:
                    gc_b(c - 1, prev)
                    cb_step(c - 1)
                prev = cur_t
            gc_b(NC8 - 1, prev)
            cb_step(NC8 - 1)
            for j in range(NCB):
                slab_state["free"][cbs[j][0]] = cbtok[0]
                cb_finish(j, cbz[j])
                zpin.discard(cbz[j])

            for j in range(NCB, 16):
                zcb = newz()
                slot, tk = load_slab(w_co_r[l, j], NC8)
                mm_group(zcb, [slab_lhs(slot, k) for k in range(NC8)], lambda k, a, b: convin[:, k, a:b],
                         reads=[("pooled", k) for k in range(NC8)], extra=[tk], slot=slot)
                cb_finish(j, zcb)

            stage_b = (p == 0 and l == 1 and not DEBUG_STOP)
            zs = newz(pin=True)
            zsB = None
            if stage_b:
                stage_next_x_load()
                zsB = newz(pin=True)
            for j in range(16):
                zo = newz()
                slot, tk = load_slab(w_out_r[l, j], NKD)
                mm_group(zo, [slab_lhs(slot, k) for k in range(NKD)], lambda k, a, b: merged_ap(k, a, b),
                         reads=[("merged", k) for k in range(NKD)] + [mkey(k) for k in range(0, NKD, 2)],
                         extra=[tk], slot=slot)
                S.emit("dve", lambda e, zo=zo, j=j: e.tensor_tensor(xres[:, j, :], xres[:, j, :], zv(zo), ALU.add),
                       reads=[("z", zo)], writes=[("xres", j)])
                if j >= 1:
                    sq_accum(zs, j - 1)
                if stage_b:
                    sqB_accum(zsB, j)
            sq_accum(zs, 15)
            if stage_b:
                return (zs, zsB)
            return zs

        for q in range(8):
            gtok = sp_dma(xres[:, 2 * q:2 * q + 2, :], xT[0, 2 * q:2 * q + 2].rearrange("c p t -> p c t"),
                          writes=[("xres", c) for c in range(2 * q, 2 * q + 2)])
            if q == 5:
                slab_state["gate"] = gtok
        zpre = None
        for l in range(2):
            if DEBUG_STOP and l == 1:
                break
            zpre = pass_layer(0, l, zpre)
        if DEBUG_STOP:
            rms_stage(True, 0, 0, zpre)
        else:
            zsA, zsB = zpre
            hist_loads(1, 0)
            tB = newtmp()
            S.emit("act", lambda e: e.activation(tap(tB), zv(zsB), AF.Sqrt, bias=RMS_EPS, scale=1.0 / D),
                   reads=[("z", zsB)], writes=[("tmp", tB)])
            S.emit("dve", lambda e: e.reciprocal(tap(TM), tap(tB)), reads=[("tmp", tB)], writes=[("tmp", TM)])
            zpin.discard(zsB)
            for c in range(NKD):
                S.emit("dve", lambda e, c=c: e.scalar_tensor_tensor(
                    hbf[:, c, :], xB(c), pcol(O_WN + c), tap(TM), ALU.mult, ALU.mult),
                    reads=xBkeys(c) + [("tmp", TM)], writes=[("h", c)])
            rms_stage(True, 0, 0, zsA)
            for c in range(NKD):
                S.emit("act", lambda e, c=c: e.copy(xres[:, c, :], xB(c)),
                       reads=xBkeys(c), writes=[("xres", c)])
            zpre = "HREADY"
            for l in range(2):
                zpre = pass_layer(1, l, zpre)
            rms_stage(True, 0, 1, zpre)

        def final_fn(e):
            return e.nop()
        final_waits = list(S.dma_tokens)

        @block.sync
        def _(e):
            S.run("sp", e, sems)
            done = {}
            for s, v in final_waits:
                done[s] = max(done.get(s, 0), v)
            for s, v in done.items():
                e.wait_ge(sems[s], v)

        @block.gpsimd
        def _(e):
            S.run("pool", e, sems)

        @block.tensor
        def _(e):
            S.run("pe", e, sems)

        @block.scalar
        def _(e):
            S.run("act", e, sems)

        @block.vector
        def _(e):
            S.run("dve", e, sems)

    return nc


def _relayout_weights(w_in, w_pool_mix, w_pool_out, w_conv_out, w_out):
    w_in_r = np.ascontiguousarray(
        w_in.reshape(2, 16, 128, 72, 128).transpose(0, 3, 2, 1, 4))
    w_po_r = np.ascontiguousarray(
        w_pool_out.reshape(2, 8, 128, 16, 128).transpose(0, 3, 2, 1, 4))
    w_co_r = np.ascontiguousarray(
        w_conv_out.reshape(2, 8, 128, 16, 128).transpose(0, 3, 2, 1, 4))
    w_out_r = np.ascontiguousarray(
        w_out.reshape(2, 16, 128, 16, 128).transpose(0, 3, 2, 1, 4))
    w_mix_r = np.ascontiguousarray(
        w_pool_mix.reshape(2, 4, 2, 128, 256).transpose(0, 3, 1, 2, 4))
    return w_in_r, w_po_r, w_co_r, w_out_r, w_mix_r


def _colmajor(v, n):
    return v.reshape(n, 128).T


def kernel(x_prompt, x_sample, cache_pool, cache_conv, w_norm, w_in, w_pool_mix, pool_scale,
           w_pool_out, conv_w, conv_b, ln_g, ln_b, w_conv_out, w_out, w_final_norm):
    f32 = np.float32
    x_prompt = np.asarray(x_prompt, f32)
    x_sample = np.asarray(x_sample, f32)
    cache_pool = np.asarray(cache_pool, f32)
    cache_conv = np.asarray(cache_conv, f32)
    w_in_r, w_po_r, w_co_r, w_out_r, w_mix_r = _relayout_weights(
        np.asarray(w_in, f32), np.asarray(w_pool_mix, f32), np.asarray(w_pool_out, f32),
        np.asarray(w_conv_out, f32), np.asarray(w_out, f32))

    prm_base = np.zeros((128, NP_), f32)
    for l in range(2):
        b = l * LP
        prm_base[:, b + O_WN:b + O_WN + 16] = _colmajor(np.asarray(w_norm, f32)[l], 16)
        prm_base[:, b + O_PS:b + O_PS + 8] = _colmajor(np.asarray(pool_scale, f32)[l], 8)
        prm_base[:, b + O_CB:b + O_CB + 8] = _colmajor(np.asarray(conv_b, f32)[l], 8)
        prm_base[:, b + O_LG:b + O_LG + 8] = _colmajor(np.asarray(ln_g, f32)[l], 8)
        prm_base[:, b + O_LB:b + O_LB + 8] = _colmajor(np.asarray(ln_b, f32)[l], 8)
        cw = np.asarray(conv_w, f32)[l]
        prm_base[:, b + O_CW:b + O_CW + 248] = cw.reshape(31, 8, 128).transpose(2, 1, 0).reshape(128, 248)
    prm_base[:, O_WF:O_WF + 16] = _colmajor(np.asarray(w_final_norm, f32), 16)

    in_maps = []
    xp = x_prompt[0]
    for i in range(NCORES):
        xa = np.zeros((T, D), f32)
        if i > 0:
            xa[0:TS] = xp[1024 * i - TS:1024 * i]
        xa[TS:] = xp[1024 * i:1024 * i + TP]
        xb = np.concatenate([x_sample[i], xp[1024 * i + TP:1024 * i + 2 * TP]], axis=0)
        xT = np.stack([xa.T.reshape(NKD, 128, T), xb.T.reshape(NKD, 128, T)], axis=0)
        hu = np.zeros((2, 2, 128, NC8, 32), f32)
        hv = np.zeros((2, 2, 128, NC8, 32), f32)
        for l in range(2):
            cu = cache_pool[l, i].T.reshape(NC8, 128, 15).transpose(1, 0, 2)
            hu[1, l, :, :, 17:32] = cu
            cv = cache_conv[l, i].T.reshape(NC8, 128, 30).transpose(1, 0, 2)
            hv[1, l, :, :, 2:32] = cv
        prm = prm_base.copy()
        for p in range(2):
            for g, w in enumerate(WINS):
                if i == 0 and p == 0:
                    cnt = np.minimum(w, np.arange(16) + 1).astype(f32)
                else:
                    cnt = np.full(16, w, f32)
                o = O_FIX + (p * 4 + g) * 16
                prm[:, o:o + 16] = (f32(1.0) / cnt)[None, :]
        in_maps.append({
            "xT": np.ascontiguousarray(xT), "histu": hu, "histv": hv, "prm": prm,
            "w_in_r": w_in_r, "w_po_r": w_po_r, "w_co_r": w_co_r, "w_out_r": w_out_r, "w_mix_r": w_mix_r,
        })

    nc = build_program()
    res = run_bass_kernel_spmd(nc, in_maps, core_ids=list(range(NCORES)))

    y_prompt = np.zeros((1, 8192, D), f32)
    y_sample = np.zeros((8, 64, D), f32)
    nps_p = np.zeros((2, 1, 15, 1024), f32)
    ncs_p = np.zeros((2, 1, 30, 1024), f32)
    nps_s = np.zeros((2, 8, 15, 1024), f32)
    ncs_s = np.zeros((2, 8, 30, 1024), f32)
    for i in range(NCORES):
        r = res.results[i]
        y = np.asarray(r["yT"], f32).reshape(2, D, T)
        y_prompt[0, 1024 * i:1024 * i + TP] = y[0][:, TS:].T
        y_prompt[0, 1024 * i + TP:1024 * i + 2 * TP] = y[1][:, TS:].T
        y_sample[i] = y[1][:, :TS].T
        us = np.asarray(r["ust"], f32)
        vs = np.asarray(r["vst"], f32)
        for l in range(2):
            nps_s[l, i] = us[l][:, :, 0, 1:16].transpose(2, 1, 0).reshape(15, 1024)
            ncs_s[l, i] = vs[l][:, :, 0, 2:32].transpose(2, 1, 0).reshape(30, 1024)
            if i == NCORES - 1:
                nps_p[l, 0] = us[l][:, :, 1, 1:16].transpose(2, 1, 0).reshape(15, 1024)
                ncs_p[l, 0] = vs[l][:, :, 1, 2:32].transpose(2, 1, 0).reshape(30, 1024)
    return (y_prompt, y_sample, nps_p, ncs_p, nps_s, ncs_s)
```

```python
import numpy as np
import concourse.bass as bass
import concourse.mybir as mybir
from concourse.bass_utils import run_bass_kernel_spmd

F32 = mybir.dt.float32
BF16 = mybir.dt.bfloat16
AF = mybir.ActivationFunctionType
ALU = mybir.AluOpType

NCORES = 8
D = 2048
T = 576
TS = 64
TP = 512
EW = 640
ES0, EP0 = 32, 128
NKD = 16
NC8 = 8
RMS_EPS = 1e-6
LN_EPS = 1e-5
WINS = (2, 4, 8, 16)

LP = 16 + 8 + 8 + 8 + 8 + 8 * 31
O_WN, O_PS, O_CB, O_LG, O_LB, O_CW = 0, 16, 24, 32, 40, 48
O_WF = 2 * LP
O_FIX = O_WF + 16
NP_ = O_FIX + 2 * 4 * 16

NSLAB = 5
NTMP = 10
DEBUG_STOP = False

ZBASE = [0, 1024, 2048, 3072]
NZ = 4
NM1 = 12
NPAIR1 = 12


def ztiles(zb):
    base = ZBASE[zb]
    return [(0, 512, base), (512, 576, base + 512)]


class Sched:
    ENGS = ("pe", "act", "dve", "pool", "sp")

    def __init__(self):
        self.ops = {e: [] for e in self.ENGS}
        self.sem = {}
        self.cnt = {}
        self.state = {}
        self.known = {e: {} for e in self.ENGS}
        self.dma_tokens = []

    def _st(self, k):
        if k not in self.state:
            self.state[k] = {"W": [], "R": []}
        return self.state[k]

    def emit(self, eng, fn, reads=(), writes=(), extra=(), semname=None, inc=1, noself=False):
        waits = {}

        def add(tok):
            if tok is None:
                return
            s, v = tok
            if waits.get(s, 0) < v:
                waits[s] = v

        for k in reads:
            for t in self._st(k)["W"]:
                add(t)
        for k in writes:
            st = self._st(k)
            for t in st["W"]:
                add(t)
            for t in st["R"]:
                add(t)
        for t in extra:
            add(t)
        sname = semname if semname is not None else eng
        if inc > 1:
            add((sname, self.cnt.get(sname, 0)))
        wl = []
        kn = self.known[eng]
        for s, v in waits.items():
            if v <= 0:
                continue
            if noself and s == eng:
                continue
            if kn.get(s, 0) >= v:
                continue
            kn[s] = v
            wl.append((s, v))
        self.cnt[sname] = self.cnt.get(sname, 0) + inc
        tok = (sname, self.cnt[sname])
        self.ops[eng].append((wl, fn, sname, inc))
        for k in reads:
            self._st(k)["R"].append(tok)
        for k in writes:
            st = self._st(k)
            st["W"] = [tok]
            st["R"] = []
        return tok

    def run(self, eng, e, sems):
        for wl, fn, sname, inc in self.ops[eng]:
            for s, v in wl:
                e.wait_ge(sems[s], v)
            inst = fn(e)
            inst.then_inc(sems[sname], inc)


def build_program():
    nc = bass.Bass("TRN2", target_bir_lowering=False, dynamic_dma_scratch_size=4096)

    def din(name, shape):
        return nc.dram_tensor(name, shape, F32, kind="ExternalInput").ap()

    def dout(name, shape):
        return nc.dram_tensor(name, shape, F32, kind="ExternalOutput").ap()

    xT = din("xT", [2, NKD, 128, T])
    histu = din("histu", [2, 2, 128, NC8, 32])
    histv = din("histv", [2, 2, 128, NC8, 32])
    prm_d = din("prm", [128, NP_])
    w_in_r = din("w_in_r", [2, 72, 128, NKD, 128])
    w_po_r = din("w_po_r", [2, 16, 128, NC8, 128])
    w_co_r = din("w_co_r", [2, 16, 128, NC8, 128])
    w_out_r = din("w_out_r", [2, 16, 128, NKD, 128])
    w_mix_r = din("w_mix_r", [2, 128, 4, 2, 256])
    yT = dout("yT", [2, NKD, 128, T])
    ust = dout("ust", [2, 128, NC8, 2, 16])
    vst = dout("vst", [2, 128, NC8, 2, 32])

    S = Sched()
    sem_names = list(Sched.ENGS) + [f"slab{i}" for i in range(NSLAB)] + ["mix"] + [f"dq{i}" for i in range(8)]

    import contextlib
    with contextlib.ExitStack() as es:
        def sb(name, shape, dt):
            return es.enter_context(nc.sbuf_tensor(name, shape, dt))

        xres = sb("xres", [128, NKD, T], F32)
        hbf = sb("hbf", [128, NKD, T], BF16)
        extu = sb("extu", [128, NC8, EW], F32)
        extv = sb("extv", [128, NC8, EW], F32)
        pooled = sb("pooled", [128, NC8, T], BF16)
        poolin = sb("poolin", [128, NC8, T], BF16)
        ring = sb("ring", [128, NSLAB, NKD * 128], BF16)
        mixs = sb("mixs", [128, 4, 2, 256], BF16)
        prm = sb("prm_s", [128, NP_], F32)
        ones = sb("ones", [128, 128], BF16)
        ustate = sb("ustate", [128, 2, NC8, 2, 16], F32)
        vstate = sb("vstate", [128, 2, NC8, 2, 32], F32)
        tmps = sb("tmps", [128, NTMP, EW], F32)
        accsb = sb("accsb", [128, 2, T], F32)
        sgm = sb("sgm", [128, 16, T], BF16)
        ps = es.enter_context(nc.psum_tensor("ps", [128, 4096], F32))
        sems = {n: es.enter_context(nc.semaphore(n)) for n in sem_names}
        block = es.enter_context(nc.Block())

        convin = pooled
        merged_flat = extu[:].bitcast(BF16)

        def merged_ap(j, a=0, b=T):
            return merged_flat[:, j // 2, (j % 2) * EW + a:(j % 2) * EW + b]

        def mkey(j):
            return ("extu", j // 2)

        tstate = {"i": 0}

        tpin = set()
        zpin = set()

        def newtmp(pin=False):
            while True:
                i = tstate["i"] % (NTMP - 2)
                tstate["i"] += 1
                if i not in tpin:
                    break
            if pin:
                tpin.add(i)
            return i

        TR = NTMP - 2
        TM = NTMP - 1

        def tap(i, a=0, b=T):
            return tmps[:, i, a:b]

        tmps_bf = tmps[:].bitcast(BF16)

        def tapb(i, a=0, b=T):
            return tmps_bf[:, i, a:b]

        def zv(zb, a=0, b=T):
            return ps[:, ZBASE[zb] + a:ZBASE[zb] + b]

        zstate = {"i": 0}

        def newz(pin=False):
            assert len(zpin) < NZ
            while True:
                z = zstate["i"] % NZ
                zstate["i"] += 1
                if z not in zpin:
                    break
            if pin:
                zpin.add(z)
            return z

        dq = {"i": 0}

        def sp_dma(out, in_, reads=(), writes=(), out_dma=False):
            q = f"dq{dq['i'] % 8}"
            dq["i"] += 1
            tok = S.emit("sp", lambda e: e.dma_start(out=out, in_=in_), reads=reads, writes=writes,
                         semname=q, inc=16)
            if out_dma:
                S.dma_tokens.append(tok)
            return tok

        slab_state = {"n": 0, "free": [None] * NSLAB}

        def load_slab(src, nk):
            n = slab_state["n"]
            slot = n % NSLAB
            slab_state["n"] += 1
            extra = [slab_state["free"][slot]]
            if n < NSLAB and slab_state.get("gate") is not None:
                extra.append(slab_state["gate"])
            dst = ring[:, slot, 0:nk * 128]
            src2 = src.rearrange("p k n -> p (k n)")
            tok = S.emit("pool", lambda e: e.dma_start(out=dst, in_=src2), extra=extra,
                         semname=f"slab{slot}", inc=16)
            return slot, tok

        def slab_lhs(slot, k):
            return ring[:, slot, k * 128:(k + 1) * 128]

        def mm_group(zb, lhs_list, rhs_fn, reads, extra=(), slot=None):
            K = len(lhs_list)
            tiles = ztiles(zb)

            def fn(e):
                last = None
                for k in range(K):
                    for (a, b, pc) in tiles:
                        last = e.matmul(ps[:, pc:pc + (b - a)], lhs_list[k], rhs_fn(k, a, b),
                                        start=(k == 0), stop=(k == K - 1))
                return last

            tok = S.emit("pe", fn, reads=reads, writes=[("z", zb)], extra=extra)
            if slot is not None:
                slab_state["free"][slot] = tok
            return tok

        S.emit("dve", lambda e: e.memset(ones[:], 1.0), writes=[("ones",)])
        S.emit("dve", lambda e: e.memset(extu[:], 0.0), writes=[("extu", c) for c in range(NC8)])
        S.emit("dve", lambda e: e.memset(extv[:], 0.0), writes=[("extv", c) for c in range(NC8)])
        S.emit("dve", lambda e: e.memset(tmps[:], 0.0), writes=[("tmp", i) for i in range(NTMP)])
        sp_dma(prm[:], prm_d[:, :], writes=[("prm",)])
        S.emit("dve", lambda e: e.tensor_copy(tmps[:, 0, 0:1], prm[:, 0:1]), reads=[("prm",)], writes=[("tmp", 0)])
        S.emit("act", lambda e: e.copy(tmps[:, 1, 0:1], prm[:, 0:1]), reads=[("prm",)], writes=[("tmp", 1)])

        def pcol(off):
            return prm[:, off:off + 1]

        def sq_accum(z, c):
            t = newtmp()
            S.emit("act", lambda e: e.activation(tapb(t), xres[:, c, :], AF.Square),
                   reads=[("xres", c)], writes=[("tmp", t)])
            tiles = ztiles(z)

            def fn(e):
                last = None
                for (a, b, pc) in tiles:
                    last = e.matmul(ps[:, pc:pc + (b - a)], ones[:], tapb(t, a, b),
                                    start=(c == 0), stop=(c == NKD - 1))
                return last
            S.emit("pe", fn, reads=[("tmp", t), ("ones",)], writes=[("z", z)])

        def rms_stage(final, l, p, zpre=None):
            z = newz() if zpre is None else zpre
            for c in range(NKD if zpre is None else 0):
                t = newtmp()
                if c % 3 != 2:
                    S.emit("act", lambda e, t=t, c=c: e.activation(tapb(t), xres[:, c, :], AF.Square),
                           reads=[("xres", c)], writes=[("tmp", t)])
                else:
                    S.emit("dve", lambda e, t=t, c=c: e.tensor_tensor(tapb(t), xres[:, c, :], xres[:, c, :], ALU.mult),
                           reads=[("xres", c)], writes=[("tmp", t)])
                tiles = ztiles(z)

                def fn(e, t=t, c=c, tiles=tiles):
                    last = None
                    for (a, b, pc) in tiles:
                        last = e.matmul(ps[:, pc:pc + (b - a)], ones[:], tapb(t, a, b),
                                        start=(c == 0), stop=(c == NKD - 1))
                    return last
                S.emit("pe", fn, reads=[("tmp", t), ("ones",)], writes=[("z", z)])
            t = newtmp()
            S.emit("act", lambda e: e.activation(tap(t), zv(z), AF.Sqrt, bias=RMS_EPS, scale=1.0 / D),
                   reads=[("z", z)], writes=[("tmp", t)])
            S.emit("dve", lambda e: e.reciprocal(tap(TR), tap(t)), reads=[("tmp", t)], writes=[("tmp", TR)])
            zpin.discard(z)
            for c in range(NKD):
                if not final:
                    S.emit("dve", lambda e, c=c: e.scalar_tensor_tensor(
                        hbf[:, c, :], xres[:, c, :], pcol(l * LP + O_WN + c), tap(TR), ALU.mult, ALU.mult),
                        reads=[("xres", c), ("tmp", TR)], writes=[("h", c)])
                else:
                    t2 = newtmp()
                    S.emit("dve", lambda e, c=c, t2=t2: e.scalar_tensor_tensor(
                        tap(t2), xres[:, c, :], pcol(O_WF + c), tap(TR), ALU.mult, ALU.mult),
                        reads=[("xres", c), ("tmp", TR)], writes=[("tmp", t2)])
                    sp_dma(yT[p, c], tap(t2), reads=[("tmp", t2)], out_dma=True)

        sgm32 = sgm[:].rearrange("p j t -> p (j t)").bitcast(F32)

        def xB(c):
            if c < NC8:
                return extv[:, c, 64:64 + T]
            return sgm32[:, (c - NC8) * T:(c - NC8 + 1) * T]

        def xBkeys(c):
            if c < NC8:
                return [("extv", c)]
            return [("sgm", 2 * (c - NC8)), ("sgm", 2 * (c - NC8) + 1)]

        def stage_next_x_load():
            for q in range(8):
                src = xT[1, 2 * q:2 * q + 2].rearrange("c p t -> p c t")
                if q < 4:
                    dst = extv[:, 2 * q:2 * q + 2, 64:64 + T]
                else:
                    dst = sgm32[:, (2 * q - NC8) * T:(2 * q - NC8 + 2) * T].rearrange("p (c t) -> p c t", c=2)
                sp_dma(dst, src, writes=xBkeys(2 * q) + xBkeys(2 * q + 1))

        def sqB_accum(z, c):
            t = newtmp()
            S.emit("act", lambda e: e.activation(tapb(t), xB(c), AF.Square),
                   reads=xBkeys(c), writes=[("tmp", t)])
            tiles = ztiles(z)

            def fn(e):
                last = None
                for (a, b, pc) in tiles:
                    last = e.matmul(ps[:, pc:pc + (b - a)], ones[:], tapb(t, a, b),
                                    start=(c == 0), stop=(c == NKD - 1))
                return last
            S.emit("pe", fn, reads=[("tmp", t), ("ones",)], writes=[("z", z)])

        def hist_loads(p, l):
            sp_dma(extu[:, :, 0:32], histu[p, l], writes=[("extu", c) for c in range(NC8)])
            sp_dma(extv[:, :, 0:32], histv[p, l], writes=[("extv", c) for c in range(NC8)])

        def pass_layer(p, l, zpre):
            PB = l * LP
            if zpre != "HREADY":
                hist_loads(p, l)
            if zpre != "HREADY":
                rms_stage(False, l, p, zpre)

            first = {"v": True}

            def win_group(m, zb):
                slot, tk = load_slab(w_in_r[l, m], NKD)
                if first["v"]:
                    first["v"] = False
                    tiles = ztiles(zb)
                    tok = None
                    for k in range(NKD):
                        def fn(e, k=k):
                            last = None
                            for (a, b, pc) in tiles:
                                last = e.matmul(ps[:, pc:pc + (b - a)], slab_lhs(slot, k), hbf[:, k, a:b],
                                                start=(k == 0), stop=(k == NKD - 1))
                            return last
                        tok = S.emit("pe", fn, reads=[("h", k)], writes=[("z", zb)], extra=[tk])
                    slab_state["free"][slot] = tok
                    return tok
                return mm_group(zb, [slab_lhs(slot, k) for k in range(NKD)],
                                lambda k, a, b: hbf[:, k, a:b],
                                reads=[("h", k) for k in range(NKD)], extra=[tk], slot=slot)

            def vfront_pe(c):
                za = newz(pin=True)
                win_group(16 + c, za)
                zb = newz()
                win_group(24 + c, zb)
                t = newtmp(pin=True)
                S.emit("act", lambda e, t=t, zb=zb: e.activation(tap(t), zv(zb), AF.Sigmoid),
                       reads=[("z", zb)], writes=[("tmp", t)])
                return (za, t)

            def vfront_dve(c, st):
                za, t = st
                S.emit("dve", lambda e: e.tensor_tensor(
                    extv[:, c, ES0:ES0 + TS], zv(za, 0, TS), tap(t, 0, TS), ALU.mult),
                    reads=[("z", za), ("tmp", t)], writes=[("extv", c)])
                S.emit("dve", lambda e: e.tensor_tensor(
                    extv[:, c, EP0:EP0 + TP], zv(za, TS, T), tap(t, TS, T), ALU.mult),
                    reads=[("z", za), ("tmp", t)], writes=[("extv", c)])
                if p == 0:
                    S.emit("act", lambda e: e.copy(extv[:, c, 96:128], extv[:, c, 64:96]),
                           reads=[], writes=[("extv", c)])
                else:
                    S.emit("act", lambda e: e.copy(extv[:, c, 96:128], vstate[:, l, c, 1, :]),
                           reads=[("vstate", l)], writes=[("extv", c)])
                S.emit("act", lambda e: e.copy(vstate[:, l, c, 0, :], extv[:, c, 64:96]),
                       reads=[("extv", c)], writes=[("vstate", l)])
                S.emit("act", lambda e: e.copy(vstate[:, l, c, 1, :], extv[:, c, 608:640]),
                       reads=[("extv", c)], writes=[("vstate", l)])
                zpin.discard(za)
                tpin.discard(t)

            def conv_taps(c, k0, k1):
                for k in range(k0, k1):
                    wk = pcol(PB + O_CW + c * 31 + k)
                    for (key, a0, n, e0) in ((("accP", c % 2), TS, TP, EP0 - 30), (("accS", c % 2), 0, TS, ES0 - 30)):
                        accap = accsb[:, c % 2, a0:a0 + n]
                        src = extv[:, c, e0 + k:e0 + k + n]
                        if k == 0:
                            S.emit("dve", lambda e, accap=accap, src=src, wk=wk: e.tensor_scalar(
                                accap, src, wk, None, ALU.mult),
                                reads=[("extv", c)], writes=[key])
                        else:
                            S.emit("dve", lambda e, accap=accap, src=src, wk=wk: e.scalar_tensor_tensor(
                                accap, src, wk, accap, ALU.mult, ALU.add),
                                reads=[("extv", c)], writes=[key])

            def v_evac(c):
                S.emit("act", lambda e: e.activation(
                    extv[:, c, 0:T], accsb[:, c % 2, 0:T], AF.Identity, bias=pcol(PB + O_CB + c), scale=1.0),
                    reads=[("accP", c % 2), ("accS", c % 2)], writes=[("extv", c)])

            def u_pe(c):
                z = newz()
                win_group(c, z)
                S.emit("act", lambda e: e.copy(extu[:, c, ES0:ES0 + TS], zv(z, 0, TS)),
                       reads=[("z", z)], writes=[("extu", c)])
                S.emit("act", lambda e: e.copy(extu[:, c, EP0:EP0 + TP], zv(z, TS, T)),
                       reads=[("z", z)], writes=[("extu", c)])
                if p == 0:
                    S.emit("act", lambda e: e.copy(extu[:, c, 112:128], extu[:, c, 80:96]),
                           reads=[], writes=[("extu", c)])
                else:
                    S.emit("act", lambda e: e.copy(extu[:, c, 112:128], ustate[:, l, c, 1, :]),
                           reads=[("ustate", l)], writes=[("extu", c)])
                S.emit("act", lambda e: e.copy(ustate[:, l, c, 0, :], extu[:, c, 80:96]),
                       reads=[("extu", c)], writes=[("ustate", l)])
                S.emit("act", lambda e: e.copy(ustate[:, l, c, 1, :], extu[:, c, 624:640]),
                       reads=[("extu", c)], writes=[("ustate", l)])
                return None

            def u_dve(c, st):
                g = c // 2
                w = WINS[g]
                cur_key = ("extu", c)
                cur = lambda a, b: extu[:, c, a:b]
                sh = 1
                while sh < w:
                    t = newtmp()
                    S.emit("dve", lambda e, t=t, cur=cur, sh=sh: e.tensor_tensor(
                        tmps[:, t, 16:EW], cur(16, EW), cur(16 - sh, EW - sh), ALU.add),
                        reads=[cur_key], writes=[("tmp", t)])
                    cur_key = ("tmp", t)
                    cur = lambda a, b, t=t: tmps[:, t, a:b]
                    sh *= 2
                for (o0, e0, n) in ((0, ES0, TS), (TS, EP0, TP)):
                    S.emit("dve", lambda e, cur=cur, o0=o0, e0=e0, n=n: e.scalar_tensor_tensor(
                        pooled[:, c, o0:o0 + n], cur(e0, e0 + n), 1.0 / w, extu[:, c, e0:e0 + n],
                        ALU.mult, ALU.subtract),
                        reads=[cur_key, ("extu", c)], writes=[("pooled", c)])
                t = newtmp()
                fo = O_FIX + (p * 4 + g) * 16
                S.emit("dve", lambda e, t=t, cur=cur: e.tensor_tensor(
                    tmps[:, t, 0:16], cur(EP0, EP0 + 16), prm[:, fo:fo + 16], ALU.mult),
                    reads=[cur_key], writes=[("tmp", t)])
                S.emit("dve", lambda e, t=t: e.tensor_tensor(
                    pooled[:, c, TS:TS + 16], tmps[:, t, 0:16], extu[:, c, EP0:EP0 + 16], ALU.subtract),
                    reads=[("tmp", t), ("extu", c)], writes=[("pooled", c)])

            mixsrc = w_mix_r[l].rearrange("p g k n -> p (g k n)")
            mixdst = mixs[:].rearrange("p g k n -> p (g k n)")
            S.emit("pool", lambda e: e.dma_start(out=mixdst, in_=mixsrc), writes=[("mixs",)],
                   semname="mix", inc=16)

            def g_pe(c):
                g = c // 2
                half = c % 2
                zg = newz()
                win_group(8 + c, zg)
                t = newtmp(pin=True)
                S.emit("act", lambda e: e.activation(tap(t), zv(zg), AF.Silu),
                       reads=[("z", zg)], writes=[("tmp", t)])
                zm = newz(pin=True)
                mm_group(zm, [mixs[:, g, kk, half * 128:(half + 1) * 128] for kk in range(2)],
                         lambda k, a, b: pooled[:, 2 * g + k, a:b],
                         reads=[("pooled", 2 * g), ("pooled", 2 * g + 1), ("mixs",)])
                return (t, zm)

            def g_dve(c, st):
                t, zm = st
                S.emit("dve", lambda e: e.scalar_tensor_tensor(
                    poolin[:, c, :], zv(zm), pcol(PB + O_PS + c), tap(t), ALU.mult, ALU.mult),
                    reads=[("z", zm), ("tmp", t)], writes=[("poolin", c)])
                zpin.discard(zm)
                tpin.discard(t)

            def pair_pe(j):
                zmp = newz()
                win_group(40 + j, zmp)
                s1 = newtmp(pin=True)
                S.emit("act", lambda e: e.activation(tap(s1), zv(zmp), AF.Sigmoid),
                       reads=[("z", zmp)], writes=[("tmp", s1)])
                zpb = newz(pin=True)
                slot, tk = load_slab(w_po_r[l, j], NC8)
                mm_group(zpb, [slab_lhs(slot, k) for k in range(NC8)], lambda k, a, b: poolin[:, k, a:b],
                         reads=[("poolin", k) for k in range(NC8)], extra=[tk], slot=slot)
                return (s1, zpb)

            def pair_dve(j, st):
                s1, zpb = st
                S.emit("dve", lambda e: e.tensor_tensor(merged_ap(j), zv(zpb), tap(s1), ALU.mult),
                       reads=[("z", zpb), ("tmp", s1)], writes=[mkey(j), ("merged", j)])
                zpin.discard(zpb)
                tpin.discard(s1)

            def c_pe(c):
                zc = newz()
                win_group(32 + c, zc)
                S.emit("act", lambda e: e.activation(pooled[:, c, :], zv(zc), AF.Silu),
                       reads=[("z", zc)], writes=[("pooled", c)])
                return None

            def c_dve(c, st):
                return None

            def m_pe(j):
                zmc = newz()
                win_group(56 + j, zmc)
                S.emit("act", lambda e: e.activation(sgm[:, j, :], zv(zmc), AF.Sigmoid),
                       reads=[("z", zmc)], writes=[("sgm", j)])
                return None

            items = [("u", c) for c in range(NC8)] + [("g", c) for c in range(NC8)] + \
                    [("p", j) for j in range(NPAIR1)] + [("c", c) for c in range(NC8)] + \
                    [("m", j) for j in range(NM1)]
            sched = [[("u", 0), ("m", 0), ("u", 1), ("m", 1), ("u", 2), ("m", 2), ("u", 3), ("m", 3),
                      ("u", 4), ("m", 4), ("m", 5)],
                     [("u", 5), ("m", 6), ("u", 6), ("m", 7), ("u", 7), ("m", 8), ("g", 0), ("g", 1), ("g", 2)],
                     [("g", 3), ("g", 4), ("g", 5), ("g", 6), ("g", 7)],
                     [("p", 0), ("c", 0), ("p", 1), ("c", 1), ("m", 9)],
                     [("p", 2), ("c", 2), ("p", 3), ("c", 3), ("m", 10)],
                     [("p", 4), ("c", 4), ("p", 5), ("c", 5), ("m", 11)],
                     [("p", 6), ("c", 6), ("p", 7), ("c", 7)],
                     [("p", 8), ("p", 9), ("p", 10), ("p", 11)]]
            assert sorted(sum(sched, [])) == sorted(items)
            PEF = {"u": u_pe, "g": g_pe, "p": pair_pe, "c": c_pe, "m": m_pe}
            DVF = {"u": u_dve, "g": g_dve, "p": pair_dve, "c": c_dve, "m": c_dve}

            st0 = vfront_pe(0)
            vfront_dve(0, st0)
            for c in range(NC8):
                F = sched[c]
                n = len(F)
                sts = [None] * n
                bounds = [round(31 * i / (n + 1)) for i in range(n + 2)]
                if c > 0:
                    v_evac(c - 1)
                sts[0] = PEF[F[0][0]](F[0][1])
                stn = None
                for i in range(n + 1):
                    conv_taps(c, bounds[i], bounds[i + 1])
                    if i >= 1:
                        DVF[F[i - 1][0]](F[i - 1][1], sts[i - 1])
                    if i + 1 < n:
                        sts[i + 1] = PEF[F[i + 1][0]](F[i + 1][1])
                    if i + 1 == max(1, (2 * n) // 3) and c + 1 < NC8:
                        stn = vfront_pe(c + 1)
                if stn is not None:
                    vfront_dve(c + 1, stn)
            v_evac(NC8 - 1)
            if p == 1:
                sp_dma(ust[l], ustate[:, l], reads=[("ustate", l)], out_dma=True)
                sp_dma(vst[l], vstate[:, l], reads=[("vstate", l)], out_dma=True)

            assert NPAIR1 == 12
            sa = pair_pe(12)
            sb_ = pair_pe(13)
            pair_dve(12, sa)
            pair_dve(13, sb_)
            sa = pair_pe(14)
            sb_ = pair_pe(15)

            z1 = newz()
            z2 = newz()
            for c in range(NC8):
                ta = newtmp()
                S.emit("act", lambda e, ta=ta, c=c: e.copy(tapb(ta), extv[:, c, 0:T]),
                       reads=[("extv", c)], writes=[("tmp", ta)])
                tb = newtmp()
                S.emit("dve", lambda e, tb=tb, c=c: e.tensor_tensor(tapb(tb), extv[:, c, 0:T], extv[:, c, 0:T], ALU.mult),
                       reads=[("extv", c)], writes=[("tmp", tb)])
                for (zz, tt) in ((z1, ta), (z2, tb)):
                    tiles = ztiles(zz)

                    def fn(e, tt=tt, c=c, tiles=tiles):
                        last = None
                        for (a, b, pc) in tiles:
                            last = e.matmul(ps[:, pc:pc + (b - a)], ones[:], tapb(tt, a, b),
                                            start=(c == 0), stop=(c == NC8 - 1))
                        return last
                    S.emit("pe", fn, reads=[("tmp", tt), ("ones",)], writes=[("z", zz)])
            pair_dve(14, sa)
            pair_dve(15, sb_)
            S.emit("act", lambda e: e.activation(tap(TM), zv(z1), AF.Copy, scale=1.0 / 1024),
                   reads=[("z", z1)], writes=[("tmp", TM)])
            tq = newtmp()
            S.emit("act", lambda e: e.activation(tap(tq), zv(z1), AF.Square, scale=1.0 / 1024),
                   reads=[("z", z1)], writes=[("tmp", tq)])
            tv = newtmp()
            S.emit("dve", lambda e: e.scalar_tensor_tensor(
                tap(tv), zv(z2), 1.0 / 1024, tap(tq), ALU.mult, ALU.subtract),
                reads=[("z", z2), ("tmp", tq)], writes=[("tmp", tv)])
            tsd = newtmp()
            S.emit("act", lambda e: e.activation(tap(tsd), tap(tv), AF.Sqrt, bias=LN_EPS, scale=1.0),
                   reads=[("tmp", tv)], writes=[("tmp", tsd)])
            S.emit("dve", lambda e: e.reciprocal(tap(TR), tap(tsd)), reads=[("tmp", tsd)], writes=[("tmp", TR)])
            for j in range(NM1, 16):
                m_pe(j)

            def gc_a(c):
                t1 = newtmp(pin=True)
                S.emit("dve", lambda e: e.tensor_tensor(tap(t1), extv[:, c, 0:T], tap(TM), ALU.subtract),
                       reads=[("extv", c), ("tmp", TM)], writes=[("tmp", t1)])
                S.emit("dve", lambda e: e.tensor_tensor(tap(t1), tap(t1), tap(TR), ALU.mult),
                       reads=[("tmp", TR)], writes=[("tmp", t1)])
                S.emit("act", lambda e: e.activation(
                    tap(t1), tap(t1), AF.Silu, bias=pcol(PB + O_LB + c), scale=pcol(PB + O_LG + c)),
                    reads=[], writes=[("tmp", t1)])
                return t1

            def gc_b(c, t1):
                S.emit("dve", lambda e: e.tensor_tensor(convin[:, c, :], tap(t1), convin[:, c, :], ALU.mult),
                       reads=[("tmp", t1)], writes=[("pooled", c)])
                tpin.discard(t1)

            NCB = 4
            cbz = [newz(pin=True) for _ in range(NCB)]
            cbs = [load_slab(w_co_r[l, j], NC8) for j in range(NCB)]
            cbtok = [None]

            def cb_step(k):
                def fn(e):
                    last = None
                    for j in range(NCB):
                        for (a, b, pc) in ztiles(cbz[j]):
                            last = e.matmul(ps[:, pc:pc + (b - a)], slab_lhs(cbs[j][0], k), convin[:, k, a:b],
                                            start=(k == 0), stop=(k == NC8 - 1))
                    return last
                cbtok[0] = S.emit("pe", fn, reads=[("pooled", k)], writes=[("z", z) for z in cbz],
                                  extra=[tk for (_, tk) in cbs])

            def cb_finish(j, zcb):
                s2 = newtmp()
                S.emit("dve", lambda e: e.tensor_tensor(tap(s2), zv(zcb), sgm[:, j, :], ALU.mult),
                       reads=[("z", zcb), ("sgm", j)], writes=[("tmp", s2)])
                S.emit("dve", lambda e: e.tensor_tensor(merged_ap(j), merged_ap(j), tap(s2), ALU.add),
                       reads=[("tmp", s2)], writes=[mkey(j), ("merged", j)])

            prev = None
            for c in range(NC8):
                cur_t = gc_a(c)
                if prev is not None:
                    gc_b(c - 1, prev)
                    cb_step(c - 1)
                prev = cur_t
            gc_b(NC8 - 1, prev)
            cb_step(NC8 - 1)
            for j in range(NCB):
                slab_state["free"][cbs[j][0]] = cbtok[0]
                cb_finish(j, cbz[j])
                zpin.discard(cbz[j])

            for j in range(NCB, 16):
                zcb = newz()
                slot, tk = load_slab(w_co_r[l, j], NC8)
                mm_group(zcb, [slab_lhs(slot, k) for k in range(NC8)], lambda k, a, b: convin[:, k, a:b],
                         reads=[("pooled", k) for k in range(NC8)], extra=[tk], slot=slot)
                cb_finish(j, zcb)

            stage_b = (p == 0 and l == 1 and not DEBUG_STOP)
            zs = newz(pin=True)
            zsB = None
            if stage_b:
                stage_next_x_load()
                zsB = newz(pin=True)
            for j in range(16):
                zo = newz()
                slot, tk = load_slab(w_out_r[l, j], NKD)
                mm_group(zo, [slab_lhs(slot, k) for k in range(NKD)], lambda k, a, b: merged_ap(k, a, b),
                         reads=[("merged", k) for k in range(NKD)] + [mkey(k) for k in range(0, NKD, 2)],
                         extra=[tk], slot=slot)
                S.emit("dve", lambda e, zo=zo, j=j: e.tensor_tensor(xres[:, j, :], xres[:, j, :], zv(zo), ALU.add),
                       reads=[("z", zo)], writes=[("xres", j)])
                if j >= 1:
                    sq_accum(zs, j - 1)
                if stage_b:
                    sqB_accum(zsB, j)
            sq_accum(zs, 15)
            if stage_b:
                return (zs, zsB)
            return zs

        for q in range(8):
            gtok = sp_dma(xres[:, 2 * q:2 * q + 2, :], xT[0, 2 * q:2 * q + 2].rearrange("c p t -> p c t"),
                          writes=[("xres", c) for c in range(2 * q, 2 * q + 2)])
            if q == 5:
                slab_state["gate"] = gtok
        zpre = None
        for l in range(2):
            if DEBUG_STOP and l == 1:
                break
            zpre = pass_layer(0, l, zpre)
        if DEBUG_STOP:
            rms_stage(True, 0, 0, zpre)
        else:
            zsA, zsB = zpre
            hist_loads(1, 0)
            tB = newtmp()
            S.emit("act", lambda e: e.activation(tap(tB), zv(zsB), AF.Sqrt, bias=RMS_EPS, scale=1.0 / D),
                   reads=[("z", zsB)], writes=[("tmp", tB)])
            S.emit("dve", lambda e: e.reciprocal(tap(TM), tap(tB)), reads=[("tmp", tB)], writes=[("tmp", TM)])
            zpin.discard(zsB)
            for c in range(NKD):
                S.emit("dve", lambda e, c=c: e.scalar_tensor_tensor(
                    hbf[:, c, :], xB(c), pcol(O_WN + c), tap(TM), ALU.mult, ALU.mult),
                    reads=xBkeys(c) + [("tmp", TM)], writes=[("h", c)])
            rms_stage(True, 0, 0, zsA)
            for c in range(NKD):
                S.emit("act", lambda e, c=c: e.copy(xres[:, c, :], xB(c)),
                       reads=xBkeys(c), writes=[("xres", c)])
            zpre = "HREADY"
            for l in range(2):
                zpre = pass_layer(1, l, zpre)
            rms_stage(True, 0, 1, zpre)

        def final_fn(e):
            return e.nop()
        final_waits = list(S.dma_tokens)

        @block.sync
        def _(e):
            S.run("sp", e, sems)
            done = {}
            for s, v in final_waits:
                done[s] = max(done.get(s, 0), v)
            for s, v in done.items():
                e.wait_ge(sems[s], v)

        @block.gpsimd
        def _(e):
            S.run("pool", e, sems)

        @block.tensor
        def _(e):
            S.run("pe", e, sems)

        @block.scalar
        def _(e):
            S.run("act", e, sems)

        @block.vector
        def _(e):
            S.run("dve", e, sems)

    return nc


def _relayout_weights(w_in, w_pool_mix, w_pool_out, w_conv_out, w_out):
    w_in_r = np.ascontiguousarray(
        w_in.reshape(2, 16, 128, 72, 128).transpose(0, 3, 2, 1, 4))
    w_po_r = np.ascontiguousarray(
        w_pool_out.reshape(2, 8, 128, 16, 128).transpose(0, 3, 2, 1, 4))
    w_co_r = np.ascontiguousarray(
        w_conv_out.reshape(2, 8, 128, 16, 128).transpose(0, 3, 2, 1, 4))
    w_out_r = np.ascontiguousarray(
        w_out.reshape(2, 16, 128, 16, 128).transpose(0, 3, 2, 1, 4))
    w_mix_r = np.ascontiguousarray(
        w_pool_mix.reshape(2, 4, 2, 128, 256).transpose(0, 3, 1, 2, 4))
    return w_in_r, w_po_r, w_co_r, w_out_r, w_mix_r


def _colmajor(v, n):
    return v.reshape(n, 128).T


def kernel(x_prompt, x_sample, cache_pool, cache_conv, w_norm, w_in, w_pool_mix, pool_scale,
           w_pool_out, conv_w, conv_b, ln_g, ln_b, w_conv_out, w_out, w_final_norm):
    f32 = np.float32
    x_prompt = np.asarray(x_prompt, f32)
    x_sample = np.asarray(x_sample, f32)
    cache_pool = np.asarray(cache_pool, f32)
    cache_conv = np.asarray(cache_conv, f32)
    w_in_r, w_po_r, w_co_r, w_out_r, w_mix_r = _relayout_weights(
        np.asarray(w_in, f32), np.asarray(w_pool_mix, f32), np.asarray(w_pool_out, f32),
        np.asarray(w_conv_out, f32), np.asarray(w_out, f32))

    prm_base = np.zeros((128, NP_), f32)
    for l in range(2):
        b = l * LP
        prm_base[:, b + O_WN:b + O_WN + 16] = _colmajor(np.asarray(w_norm, f32)[l], 16)
        prm_base[:, b + O_PS:b + O_PS + 8] = _colmajor(np.asarray(pool_scale, f32)[l], 8)
        prm_base[:, b + O_CB:b + O_CB + 8] = _colmajor(np.asarray(conv_b, f32)[l], 8)
        prm_base[:, b + O_LG:b + O_LG + 8] = _colmajor(np.asarray(ln_g, f32)[l], 8)
        prm_base[:, b + O_LB:b + O_LB + 8] = _colmajor(np.asarray(ln_b, f32)[l], 8)
        cw = np.asarray(conv_w, f32)[l]
        prm_base[:, b + O_CW:b + O_CW + 248] = cw.reshape(31, 8, 128).transpose(2, 1, 0).reshape(128, 248)
    prm_base[:, O_WF:O_WF + 16] = _colmajor(np.asarray(w_final_norm, f32), 16)

    in_maps = []
    xp = x_prompt[0]
    for i in range(NCORES):
        xa = np.zeros((T, D), f32)
        if i > 0:
            xa[0:TS] = xp[1024 * i - TS:1024 * i]
        xa[TS:] = xp[1024 * i:1024 * i + TP]
        xb = np.concatenate([x_sample[i], xp[1024 * i + TP:1024 * i + 2 * TP]], axis=0)
        xT = np.stack([xa.T.reshape(NKD, 128, T), xb.T.reshape(NKD, 128, T)], axis=0)
        hu = np.zeros((2, 2, 128, NC8, 32), f32)
        hv = np.zeros((2, 2, 128, NC8, 32), f32)
        for l in range(2):
            cu = cache_pool[l, i].T.reshape(NC8, 128, 15).transpose(1, 0, 2)
            hu[1, l, :, :, 17:32] = cu
            cv = cache_conv[l, i].T.reshape(NC8, 128, 30).transpose(1, 0, 2)
            hv[1, l, :, :, 2:32] = cv
        prm = prm_base.copy()
        for p in range(2):
            for g, w in enumerate(WINS):
                if i == 0 and p == 0:
                    cnt = np.minimum(w, np.arange(16) + 1).astype(f32)
                else:
                    cnt = np.full(16, w, f32)
                o = O_FIX + (p * 4 + g) * 16
                prm[:, o:o + 16] = (f32(1.0) / cnt)[None, :]
        in_maps.append({
            "xT": np.ascontiguousarray(xT), "histu": hu, "histv": hv, "prm": prm,
            "w_in_r": w_in_r, "w_po_r": w_po_r, "w_co_r": w_co_r, "w_out_r": w_out_r, "w_mix_r": w_mix_r,
        })

    nc = build_program()
    res = run_bass_kernel_spmd(nc, in_maps, core_ids=list(range(NCORES)))

    y_prompt = np.zeros((1, 8192, D), f32)
    y_sample = np.zeros((8, 64, D), f32)
    nps_p = np.zeros((2, 1, 15, 1024), f32)
    ncs_p = np.zeros((2, 1, 30, 1024), f32)
    nps_s = np.zeros((2, 8, 15, 1024), f32)
    ncs_s = np.zeros((2, 8, 30, 1024), f32)
    for i in range(NCORES):
        r = res.results[i]
        y = np.asarray(r["yT"], f32).reshape(2, D, T)
        y_prompt[0, 1024 * i:1024 * i + TP] = y[0][:, TS:].T
        y_prompt[0, 1024 * i + TP:1024 * i + 2 * TP] = y[1][:, TS:].T
        y_sample[i] = y[1][:, :TS].T
        us = np.asarray(r["ust"], f32)
        vs = np.asarray(r["vst"], f32)
        for l in range(2):
            nps_s[l, i] = us[l][:, :, 0, 1:16].transpose(2, 1, 0).reshape(15, 1024)
            ncs_s[l, i] = vs[l][:, :, 0, 2:32].transpose(2, 1, 0).reshape(30, 1024)
            if i == NCORES - 1:
                nps_p[l, 0] = us[l][:, :, 1, 1:16].transpose(2, 1, 0).reshape(15, 1024)
                ncs_p[l, 0] = vs[l][:, :, 1, 2:32].transpose(2, 1, 0).reshape(30, 1024)
    return (y_prompt, y_sample, nps_p, ncs_p, nps_s, ncs_s)
```
